# Optimizing a Trainium2 kernel written in Bass

```python
import math
import jax, jax.numpy as jnp
from jax import lax
import numpy as np

D_MODEL = 1024
BATCH = 8
SEQ = 8192
DEPTH = 1

SSM_GROUP_SIZE = 16
SSM_GROUPS = 32
SSM_WIDTH = SSM_GROUP_SIZE * SSM_GROUPS
SSM_STATE = 64
SSM_CHUNK = 128
SSM_DT_MIN = 1e-3
SSM_DT_MAX = 1e-1
ATTN_PATTERNS = ((128, 1), (512, 4), (2048, 16))
ATTN_HEADS_PER_GROUP = 4
ATTN_HEAD_DIM = 64
ATTN_HEADS = ATTN_HEADS_PER_GROUP * len(ATTN_PATTERNS)
ATTN_WIDTH = ATTN_HEADS * ATTN_HEAD_DIM
ATTN_OUT_WIDTH = ATTN_HEADS_PER_GROUP * ATTN_HEAD_DIM
MEM_LEN = 256
MEM_HEADS = 4
MEM_HEAD_DIM = 128
MEM_WIDTH = MEM_HEADS * MEM_HEAD_DIM
N_BRANCHES = 3
D_FF = 4 * D_MODEL
IN_SPLITS = (SSM_WIDTH, ATTN_WIDTH, ATTN_WIDTH, ATTN_WIDTH, MEM_WIDTH, N_BRANCHES * D_MODEL)
IN_WIDTH = sum(IN_SPLITS)
IN_OFFSETS = tuple(int(o) for o in np.cumsum(IN_SPLITS)[:-1])
RMS_EPS = 1e-6
NEG_INF = -1e30

kernel_name = "hybrid_s5_dilated_attn_memory_gated_block"


def rms_norm(x, g):
    xf = x.astype(jnp.float32)
    y = xf * lax.rsqrt(jnp.mean(xf * xf, axis=-1, keepdims=True) + RMS_EPS)
    return (y * g.astype(jnp.float32)).astype(x.dtype)


def _complex_affine_combine(e1, e2):
    a1r, a1i, b1r, b1i = e1
    a2r, a2i, b2r, b2i = e2
    ar = a2r * a1r - a2i * a1i
    ai = a2r * a1i + a2i * a1r
    br = a2r * b1r - a2i * b1i + b2r
    bi = a2r * b1i + a2i * b1r + b2i
    return ar, ai, br, bi


def s5_ssm(u, lam_re, lam_im, log_dt, b_re, b_im, c_re, c_im, d_skip):
    f32 = jnp.float32
    bsz, l, _ = u.shape
    n_chunks = l // SSM_CHUNK
    u = u.astype(f32).reshape(bsz, n_chunks, SSM_CHUNK, SSM_GROUPS, SSM_GROUP_SIZE)
    u = u.transpose(1, 0, 2, 3, 4)
    lr, li = lam_re.astype(f32), lam_im.astype(f32)
    dt = jnp.exp(log_dt.astype(f32))[:, None]
    mag = jnp.exp(lr * dt)
    a_re, a_im = mag * jnp.cos(li * dt), mag * jnp.sin(li * dt)
    nr, ni = a_re - 1.0, a_im
    den = lr * lr + li * li
    coef_re = (nr * lr + ni * li) / den
    coef_im = (ni * lr - nr * li) / den
    br_, bi_ = b_re.astype(f32), b_im.astype(f32)
    bb_re = coef_re[..., None] * br_ - coef_im[..., None] * bi_
    bb_im = coef_re[..., None] * bi_ + coef_im[..., None] * br_
    cr, ci = c_re.astype(f32), c_im.astype(f32)
    dd = d_skip.astype(f32)

    def chunk_step(carry, u_c):
        s_re0, s_im0 = carry
        bu_re = jnp.einsum('bcgh,gph->bcgp', u_c, bb_re)
        bu_im = jnp.einsum('bcgh,gph->bcgp', u_c, bb_im)
        ar = jnp.broadcast_to(a_re, bu_re.shape)
        ai = jnp.broadcast_to(a_im, bu_re.shape)
        pr, pi, hr, hi = lax.associative_scan(_complex_affine_combine, (ar, ai, bu_re, bu_im), axis=1)
        s_re = hr + pr * s_re0[:, None] - pi * s_im0[:, None]
        s_im = hi + pr * s_im0[:, None] + pi * s_re0[:, None]
        y = (jnp.einsum('bcgp,ghp->bcgh', s_re, cr)
             - jnp.einsum('bcgp,ghp->bcgh', s_im, ci)
             + dd * u_c)
        return (s_re[:, -1], s_im[:, -1]), y

    init = (jnp.zeros((bsz, SSM_GROUPS, SSM_STATE), f32), jnp.zeros((bsz, SSM_GROUPS, SSM_STATE), f32))
    _, y = lax.scan(chunk_step, init, u)
    return y.transpose(1, 0, 2, 3, 4).reshape(bsz, l, SSM_WIDTH)


def dilated_window_attention(q, k, v, window, dilation):
    f32 = jnp.float32
    b, l, h, e = q.shape
    w = window // dilation
    m = l // dilation
    nb = -(-m // w)
    mp = nb * w

    def to_sub(t):
        t = t.reshape(b, m, dilation, h, e).transpose(0, 2, 3, 1, 4)
        t = jnp.pad(t, ((0, 0), (0, 0), (0, 0), (0, mp - m), (0, 0)))
        return t.reshape(b, dilation, h, nb, w, e)

    def with_prev(t):
        prev = jnp.pad(t, ((0, 0), (0, 0), (0, 0), (1, 0), (0, 0), (0, 0)))[:, :, :, :-1]
        return jnp.concatenate([prev, t], axis=4)

    qs, ks, vs = to_sub(q), to_sub(k), to_sub(v)
    kb, vb = with_prev(ks), with_prev(vs)
    s = jnp.einsum('bdhnqe,bdhnke->bdhnqk', qs, kb).astype(f32) * (e ** -0.5)
    qi = jnp.arange(w)[:, None]
    kj = jnp.arange(2 * w)[None, :]
    dist = w + qi - kj
    blk = jnp.arange(nb)[:, None, None]
    valid = (dist >= 0) & (dist <= w) & ((blk > 0) | (kj >= w))
    s = jnp.where(valid, s, NEG_INF)
    mx = jnp.max(s, axis=-1, keepdims=True)
    p = jnp.exp(s - mx)
    den = jnp.sum(p, axis=-1)
    o = jnp.einsum('bdhnqk,bdhnke->bdhnqe', p, vb.astype(f32)) / den[..., None]
    lse = mx[..., 0] + jnp.log(den)
    o = o.reshape(b, dilation, h, mp, e)[:, :, :, :m].transpose(0, 3, 1, 2, 4).reshape(b, l, h, e)
    lse = lse.reshape(b, dilation, h, mp)[..., :m].transpose(0, 3, 1, 2).reshape(b, l, h)
    return o, lse


def memory_cross_attention(q, k, v):
    b, l, hm, e = q.shape
    s = jnp.einsum('blhe,bmhe->bhlm', q, k).astype(jnp.float32) * (e ** -0.5)
    p = jax.nn.softmax(s, axis=-1)
    o = jnp.einsum('bhlm,bmhe->blhe', p, v.astype(jnp.float32))
    return o.astype(q.dtype).reshape(b, l, hm * e)


def setup_inputs(seed: int = 0) -> dict:
    key = jax.random.key(seed)
    ks = jax.random.split(key, 26)
    f32 = jnp.float32

    def nrm(k, shape, scale):
        return jax.random.normal(k, shape, f32) * scale

    G, P, H = SSM_GROUPS, SSM_STATE, SSM_GROUP_SIZE
    return {
        "x": nrm(ks[0], (BATCH, SEQ, D_MODEL), 1.0),
        "mem": nrm(ks[1], (BATCH, MEM_LEN, D_MODEL), 1.0),
        "norm1_g": 1.0 + nrm(ks[2], (DEPTH, D_MODEL), 0.02),
        "mem_norm_g": 1.0 + nrm(ks[3], (DEPTH, D_MODEL), 0.02),
        "w_in": nrm(ks[4], (DEPTH, D_MODEL, IN_WIDTH), D_MODEL ** -0.5),
        "b_gate": nrm(ks[5], (DEPTH, N_BRANCHES * D_MODEL), 0.02),
        "ssm_lambda_re": -0.5 + nrm(ks[6], (DEPTH, G, P), 0.01),
        "ssm_lambda_im": math.pi * jnp.arange(P, dtype=f32) + nrm(ks[7], (DEPTH, G, P), 0.01),
        "ssm_log_dt": jax.random.uniform(ks[8], (DEPTH, G), f32, math.log(SSM_DT_MIN), math.log(SSM_DT_MAX)),
        "ssm_b_re": nrm(ks[9], (DEPTH, G, P, H), (2 * H) ** -0.5),
        "ssm_b_im": nrm(ks[10], (DEPTH, G, P, H), (2 * H) ** -0.5),
        "ssm_c_re": nrm(ks[11], (DEPTH, G, H, P), P ** -0.5),
        "ssm_c_im": nrm(ks[12], (DEPTH, G, H, P), P ** -0.5),
        "ssm_d": nrm(ks[13], (DEPTH, G, H), 1.0),
        "w_glu": nrm(ks[14], (DEPTH, SSM_WIDTH, SSM_WIDTH), SSM_WIDTH ** -0.5),
        "b_glu": nrm(ks[15], (DEPTH, SSM_WIDTH), 0.02),
        "w_ssm_br": nrm(ks[16], (DEPTH, SSM_WIDTH, D_MODEL), SSM_WIDTH ** -0.5),
        "w_attn_br": nrm(ks[17], (DEPTH, ATTN_OUT_WIDTH, D_MODEL), ATTN_OUT_WIDTH ** -0.5),
        "w_mem_kv": nrm(ks[18], (DEPTH, D_MODEL, 2 * MEM_WIDTH), D_MODEL ** -0.5),
        "w_mem_br": nrm(ks[19], (DEPTH, MEM_WIDTH, D_MODEL), MEM_WIDTH ** -0.5),
        "w_o": nrm(ks[20], (DEPTH, D_MODEL, D_MODEL), D_MODEL ** -0.5),
        "norm2_g": 1.0 + nrm(ks[21], (DEPTH, D_MODEL), 0.02),
        "w_up": nrm(ks[22], (DEPTH, D_MODEL, D_FF), D_MODEL ** -0.5),
        "w_down": nrm(ks[23], (DEPTH, D_FF, D_MODEL), D_FF ** -0.5),
        "final_g": 1.0 + nrm(ks[24], (D_MODEL,), 0.02),
    }


def reference(x, mem, norm1_g, mem_norm_g, w_in, b_gate, ssm_lambda_re, ssm_lambda_im, ssm_log_dt,
              ssm_b_re, ssm_b_im, ssm_c_re, ssm_c_im, ssm_d, w_glu, b_glu, w_ssm_br, w_attn_br,
              w_mem_kv, w_mem_br, w_o, norm2_g, w_up, w_down, final_g):
    bsz, seq, _ = x.shape
    h = x
    for i in range(DEPTH):
        n = rms_norm(h, norm1_g[i])
        z = n @ w_in[i]
        u, q, k, v, mq, zg = jnp.split(z, IN_OFFSETS, axis=-1)
        gates = jax.nn.sigmoid(zg + b_gate[i]).reshape(bsz, seq, N_BRANCHES, D_MODEL)

        y = s5_ssm(u, ssm_lambda_re[i], ssm_lambda_im[i], ssm_log_dt[i], ssm_b_re[i], ssm_b_im[i],
                   ssm_c_re[i], ssm_c_im[i], ssm_d[i]).astype(x.dtype)
        y = jax.nn.gelu(y)
        y = y * jax.nn.sigmoid(y @ w_glu[i] + b_glu[i])
        br_ssm = y @ w_ssm_br[i]

        q = q.reshape(bsz, seq, ATTN_HEADS, ATTN_HEAD_DIM)
        k = k.reshape(bsz, seq, ATTN_HEADS, ATTN_HEAD_DIM)
        v = v.reshape(bsz, seq, ATTN_HEADS, ATTN_HEAD_DIM)
        outs, lses = [], []
        for g, (window, dilation) in enumerate(ATTN_PATTERNS):
            sl = slice(g * ATTN_HEADS_PER_GROUP, (g + 1) * ATTN_HEADS_PER_GROUP)
            o_g, lse_g = dilated_window_attention(q[:, :, sl], k[:, :, sl], v[:, :, sl], window, dilation)
            outs.append(o_g)
            lses.append(lse_g)
        wts = jax.nn.softmax(jnp.stack(lses, axis=0), axis=0)
        o = jnp.sum(wts[..., None] * jnp.stack(outs, axis=0), axis=0)
        br_attn = o.astype(x.dtype).reshape(bsz, seq, ATTN_OUT_WIDTH) @ w_attn_br[i]

        kv = rms_norm(mem, mem_norm_g[i]) @ w_mem_kv[i]
        mk, mv = jnp.split(kv, 2, axis=-1)
        mk = mk.reshape(bsz, MEM_LEN, MEM_HEADS, MEM_HEAD_DIM)
        mv = mv.reshape(bsz, MEM_LEN, MEM_HEADS, MEM_HEAD_DIM)
        mq = mq.reshape(bsz, seq, MEM_HEADS, MEM_HEAD_DIM)
        br_mem = memory_cross_attention(mq, mk, mv) @ w_mem_br[i]

        merged = gates[:, :, 0] * br_ssm + gates[:, :, 1] * br_attn + gates[:, :, 2] * br_mem
        h = h + merged @ w_o[i]

        n2 = rms_norm(h, norm2_g[i])
        h = h + jnp.square(jax.nn.relu(n2 @ w_up[i])) @ w_down[i]
    return rms_norm(h, final_g)
```

```python
import numpy as np
import ml_dtypes
import concourse.bass as bass
import concourse.mybir as mybir
from concourse.bass_utils import run_bass_kernel_spmd

F32 = mybir.dt.float32
BF16 = mybir.dt.bfloat16
AF = mybir.ActivationFunctionType
ALU = mybir.AluOpType

D = 1024
SEQ = 8192
NB = 8
INW = 6400
DFF = 4096
MEM = 256
EPS = 1e-6
TT = 512


class Buf:
    __slots__ = ("w", "r")

    def __init__(self):
        self.w = None
        self.r = {}


def bufs(n):
    return [Buf() for _ in range(n)]


class KB:
    def __init__(self, nc):
        self.nc = nc
        self.eng = {"pe": nc.tensor, "act": nc.scalar, "dve": nc.vector,
                    "pool": nc.gpsimd, "sp": nc.sync}
        self.sem = {e: nc.alloc_semaphore("s_" + e) for e in self.eng}
        self.cnt = {e: 0 for e in self.eng}
        self.seen = {e: {} for e in self.eng}
        self.dq = {}
        for q in ("sp", "pool", "act"):
            self.dq[q] = [[("d_%s%d" % (q, i)), nc.alloc_semaphore("d_%s%d" % (q, i)), 0]
                          for i in range(6)]
        self.dqi = {q: 0 for q in self.dq}
        self.ndma = 0

    def _deps(self, rd, wr):
        deps = []
        for b in rd:
            if b.w is not None:
                deps.append(b.w)
        for b in wr:
            if b.w is not None:
                deps.append(b.w)
            deps.extend(b.r.values())
        return deps

    def _wait(self, e, deps):
        seen = self.seen[e]
        eng = self.eng[e]
        for (sname, sem, val) in deps:
            if seen.get(sname, 0) < val:
                eng.wait_ge(sem, val)
                seen[sname] = val

    def op(self, e, fn, rd=(), wr=()):
        deps = self._deps(rd, wr)
        if e == "pe":
            deps = [d_ for d_ in deps if d_[0] != "s_pe"]
        self._wait(e, deps)
        inst = fn(self.eng[e])
        self.cnt[e] += 1
        inst.then_inc(self.sem[e], 1)
        tok = ("s_" + e, self.sem[e], self.cnt[e])
        for b in wr:
            b.w = tok
            b.r = {}
        for b in rd:
            b.r[tok[0]] = tok
        return tok

    def dma(self, q, out, in_, rd=(), wr=(), **kw):
        deps = self._deps(rd, wr)
        slots = self.dq[q]
        i = self.dqi[q]
        self.dqi[q] = (i + 1) % len(slots)
        s = slots[i]
        if s[2] > 0:
            deps.append((s[0], s[1], 16 * s[2]))
        self._wait(q, deps)
        inst = self.eng[q].dma_start(out=out, in_=in_, **kw)
        inst.then_inc(s[1], 16)
        s[2] += 1
        tok = (s[0], s[1], 16 * s[2])
        for b in wr:
            b.w = tok
            b.r = {}
        for b in rd:
            b.r[tok[0]] = tok
        self.ndma += 1
        return tok

    def wait_all(self, e, toks):
        self._wait(e, toks)

    def all_toks(self):
        toks = []
        for q in self.dq:
            for s in self.dq[q]:
                if s[2] > 0:
                    toks.append((s[0], s[1], 16 * s[2]))
        for e in self.eng:
            if self.cnt[e] > 0:
                toks.append(("s_" + e, self.sem[e], self.cnt[e]))
        return toks

    def barrier(self):
        toks = self.all_toks()
        for e in self.eng:
            self._wait(e, toks)


def build(L=SEQ, phases="A1234C", debug=False):
    nc = bass.Bass("TRN2", target_bir_lowering=False)
    kb = KB(nc)
    NT = L // TT

    def din(name, shape, dt=F32):
        return nc.dram_tensor(name, list(shape), dt, kind="ExternalInput").ap()

    x = din("x", [L, D])
    w_in = din("w_in", [D, INW])
    g1rep = din("g1rep", [128, D])
    g2rep = din("g2rep", [128, D])
    gfrep = din("gfrep", [128, D])
    bgT = din("bgT", [128, 24])
    w_up = din("w_up", [D, DFF])
    w_down = din("w_down", [DFF, D])
    ident_d = din("ident", [128, 128])
    mem_d = din("mem", [MEM, D])
    gmrep = din("gmrep", [128, D])
    w_mem_kv = din("w_mem_kv", [D, 1024])
    w_glu = din("w_glu", [512, 512])
    bgluT = din("bgluT", [128, 4])
    w_ssm_br = din("w_ssm_br", [512, D])
    w_attn_br = din("w_attn_br", [256, D])
    w_mem_br = din("w_mem_br", [512, D])
    w_o = din("w_o", [D, D])
    maskcp_d = din("maskcp", [128, 512])
    lamC_re_d = din("lamC_re", [128, 4, 64]); lamC_im_d = din("lamC_im", [128, 4, 64])
    dtC_d = din("dtC", [128, 4])
    bC_re_d = din("bC_re", [128, 4, 64]); bC_im_d = din("bC_im", [128, 4, 64])
    dcolC_d = din("dcolC", [128, 4])
    k7_d = din("k7", [128, 8]); k9_d = din("k9", [128, 9])
    eo_d = din("evenodd", [128, 4])
    bmask_d = din("bmask", [128, 8])
    lamP_re_d = din("lamP_re", [128, 32]); lamP_im_d = din("lamP_im", [128, 32])
    dtP_d = din("dtP", [128, 32])
    cU_d = din("cU", [128, 32, 16]); cW_d = din("cW", [128, 32, 16])
    sgn_d = din("sgn", [128, 1])
    pswap_d = din("pswap", [128, 128])
    self_d = din("selfm", [128, 64])
    out = nc.dram_tensor("out", [L, D], F32, kind="ExternalOutput").ap()
    zT = nc.dram_tensor("zT", [INW, L], BF16, kind="ExternalOutput" if debug else "Internal").ap()
    hscr = nc.dram_tensor("hscr", [L, D], F32, kind="ExternalOutput" if debug else "Internal").ap()
    brT = nc.dram_tensor("brT", [1280, L], BF16, kind="ExternalOutput" if debug else "Internal").ap()

    def sb(name, shape, dt):
        return nc.alloc_sbuf_tensor(name, list(shape), dt).ap()

    identf = sb("identf", [128, 128], F32)
    identb = sb("identb", [128, 128], BF16)
    b_identf, b_identb = Buf(), Buf()
    kb.dma("sp", identf[:], ident_d[:, :], wr=[b_identf])
    kb.op("dve", lambda e: e.tensor_copy(identb[:], identf[:]), rd=[b_identf], wr=[b_identb])

    NPS = 8
    ps = [nc.alloc_psum_tensor("ps%d" % i, [128, 512], F32).ap() for i in range(NPS)]
    b_ps = bufs(NPS)
    psi = [0]

    def next_ps():
        i = psi[0]
        psi[0] = (i + 1) % NPS
        return ps[i], b_ps[i]

    ss = sb("ss", [128, 8], F32)
    rstd = sb("rstd", [128, 8], F32)
    b_ss, b_rstd = Buf(), Buf()
    junk = sb("junk", [128, D], BF16)
    b_junk = Buf()
    epsc = sb("epsc", [128, 1], F32)
    b_eps = Buf()
    kb.op("dve", lambda e: e.memset(epsc[:], EPS), wr=[b_eps])

    def rms_stats(xt, b_xt, nblk):
        for b in range(nblk):
            kb.op("act", lambda e, b=b: e.activation(junk[:], xt[:, b, :], AF.Square,
                                                     accum_out=ss[:, b:b + 1]),
                  rd=[b_xt], wr=[b_junk, b_ss])
        kb.op("act", lambda e: e.activation(rstd[:, 0:nblk], ss[:, 0:nblk], AF.Sqrt,
                                            bias=epsc[:, 0:1], scale=1.0 / D),
              rd=[b_ss, b_eps], wr=[b_rstd])
        kb.op("dve", lambda e: e.reciprocal(rstd[:, 0:nblk], rstd[:, 0:nblk]),
              rd=[b_rstd], wr=[b_rstd])

    def norm_to_T(xt, b_xt, grep, b_grep, nb_t, b_nb, nT, b_nT, nblk=4):
        rms_stats(xt, b_xt, nblk)
        for b in range(nblk):
            kb.op("dve", lambda e, b=b: e.scalar_tensor_tensor(
                out=nb_t[:, b, :], in0=xt[:, b, :], scalar=rstd[:, b:b + 1], in1=grep[:],
                op0=ALU.mult, op1=ALU.mult), rd=[b_xt, b_rstd, b_grep], wr=[b_nb])
        for ct in range(8):
            p, bp = next_ps()
            pb = p.bitcast(BF16)
            for b in range(nblk):
                kb.op("pe", lambda e, b=b, ct=ct, pb=pb: e.transpose(
                    pb[:, b * 128:(b + 1) * 128], nb_t[:, b, ct * 128:(ct + 1) * 128], identb[:]),
                    rd=[b_nb, b_identb], wr=[bp])
            eng = "act" if ct % 2 == 0 else "dve"
            if eng == "act":
                kb.op("act", lambda e, ct=ct, pb=pb: e.copy(nT[:, ct, :], pb[:, 0:nblk * 128]),
                      rd=[bp], wr=[b_nT[ct]])
            else:
                kb.op("dve", lambda e, ct=ct, pb=pb: e.tensor_copy(nT[:, ct, :], pb[:, 0:nblk * 128]),
                      rd=[bp], wr=[b_nT[ct]])

    mark = (nc.sbuf_base, nc.sbuf_top)

    def phase_end():
        kb.barrier()
        nc.sbuf_base, nc.sbuf_top = mark

    if "A" in phases:
        win = sb("win", [128, 8, INW], BF16)
        b_win = bufs(8)
        WCH = 640
        b_winc = bufs(INW // WCH)
        for cc in range(INW // WCH):
            kb.dma("pool", win[:, :, cc * WCH:(cc + 1) * WCH],
                   w_in[:, cc * WCH:(cc + 1) * WCH].rearrange("(k p) c -> p k c", p=128), wr=[b_winc[cc]],
                   max_dma_last_dim=2560)
        g1 = sb("g1", [128, D], F32)
        b_g1 = Buf()
        kb.dma("sp", g1[:], g1rep[:, :], wr=[b_g1])
        bg = sb("bg", [128, 24], F32)
        b_bg = Buf()
        kb.dma("sp", bg[:], bgT[:, :], wr=[b_bg])
        xts = [sb("xtA%d" % i, [128, 4, D], F32) for i in range(2)]
        b_xts = bufs(2)
        nbts = [sb("nbtA%d" % i, [128, 4, D], BF16) for i in range(2)]
        b_nbts = bufs(2)
        nTs = [sb("nTA%d" % i, [128, 8, TT], BF16) for i in range(2)]
        b_nTs = [bufs(8) for _ in range(2)]
        NST = 4
        zst = [sb("zst%d" % i, [128, TT], BF16) for i in range(NST)]
        b_zst = bufs(NST)

        def load_x(t):
            kb.dma("sp", xts[t % 2][:], x[t * TT:(t + 1) * TT, :].rearrange("(b p) c -> p b c", p=128),
                   wr=[b_xts[t % 2]])

        def normA(t):
            norm_to_T(xts[t % 2], b_xts[t % 2], g1, b_g1, nbts[t % 2], b_nbts[t % 2], nTs[t % 2], b_nTs[t % 2])

        load_x(0)
        if NT > 1:
            load_x(1)
        normA(0)
        zi = 0
        for t in range(NT):
            nT, b_nT = nTs[t % 2], b_nTs[t % 2]
            for co in range(INW // 128):
                if co == 6 and t + 1 < NT:
                    normA(t + 1)
                if co == 20 and t + 2 < NT:
                    load_x(t + 2)
                p, bp = next_ps()
                for kt in range(8):
                    kb.op("pe", lambda e, kt=kt, co=co, p=p: e.matmul(
                        p[:], win[:, kt, co * 128:(co + 1) * 128], nT[:, kt, :],
                        start=(kt == 0), stop=(kt == 7)),
                        rd=[b_winc[(co * 128) // WCH], b_nT[kt]], wr=[bp])
                z, bz = zst[zi % NST], b_zst[zi % NST]
                zi += 1
                if co >= 26:
                    g = co - 26
                    kb.op("act", lambda e, z=z, p=p, g=g: e.activation(
                        z[:], p[:], AF.Sigmoid, bias=bg[:, g:g + 1], scale=1.0),
                        rd=[bp, b_bg], wr=[bz])
                else:
                    kb.op("dve", lambda e, z=z, p=p: e.tensor_copy(z[:], p[:]), rd=[bp], wr=[bz])
                kb.dma("sp", zT[co * 128:(co + 1) * 128, t * TT:(t + 1) * TT], z[:], rd=[bz])
        phase_end()

    if "1" in phases:
        wkv = sb("wkv", [128, 8, 1024], BF16)
        b_wkv = bufs(8)
        for kt in range(8):
            kb.dma("pool", wkv[:, kt, :], w_mem_kv[kt * 128:(kt + 1) * 128, :], wr=[b_wkv[kt]],
                   max_dma_last_dim=4096)
        gm = sb("gm", [128, D], F32)
        b_gm = Buf()
        kb.dma("sp", gm[:], gmrep[:, :], wr=[b_gm])
        memt = sb("memt", [128, 2, D], F32)
        b_memt = Buf()
        kb.dma("sp", memt[:], mem_d[:, :].rearrange("(b p) c -> p b c", p=128), wr=[b_memt])
        nbm = sb("nbm", [128, 2, D], BF16)
        b_nbm = Buf()
        nmT = sb("nmT", [128, 8, 256], BF16)
        b_nmT = bufs(8)
        norm_to_T(memt, b_memt, gm, b_gm, nbm, b_nbm, nmT, b_nmT, nblk=2)
        mkT = sb("mkT", [128, 4, 256], BF16)
        b_mkT = Buf()
        mv = sb("mv", [128, 2, 512], BF16)
        b_mv = Buf()
        for h in range(4):
            p, bp = next_ps()
            for kt in range(8):
                kb.op("pe", lambda e, kt=kt, h=h, p=p: e.matmul(
                    p[:, 0:256], wkv[:, kt, h * 128:(h + 1) * 128], nmT[:, kt, :],
                    start=(kt == 0), stop=(kt == 7)), rd=[b_wkv[kt], b_nmT[kt]], wr=[bp])
            kb.op("dve", lambda e, h=h, p=p: e.tensor_copy(mkT[:, h, :], p[:, 0:256]), rd=[bp], wr=[b_mkT])
        for mt in range(2):
            p, bp = next_ps()
            for kt in range(8):
                kb.op("pe", lambda e, kt=kt, mt=mt, p=p: e.matmul(
                    p[:], nmT[:, kt, mt * 128:(mt + 1) * 128], wkv[:, kt, 512:1024],
                    start=(kt == 0), stop=(kt == 7)), rd=[b_wkv[kt], b_nmT[kt]], wr=[bp])
            kb.op("dve", lambda e, mt=mt, p=p: e.tensor_copy(mv[:, mt, :], p[:]), rd=[bp], wr=[b_mv])
        onesb = sb("onesb", [128, 128], BF16)
        b_ones = Buf()
        kb.op("dve", lambda e: e.memset(onesb[:], 1.0), wr=[b_ones])
        mqs = [sb("mq%d" % i, [128, 4, TT], BF16) for i in range(2)]
        b_mqs = bufs(2)
        rdn = [sb("rdn%d" % i, [128, TT], F32) for i in range(2)]
        b_rdn = bufs(2)
        mo = [sb("mo%d" % i, [128, TT], BF16) for i in range(2)]
        b_mo = bufs(2)
        MQ0 = 512 + 3 * 768

        def load_mq(t):
            kb.dma("sp", mqs[t % 2][:], zT[MQ0:MQ0 + 512, t * TT:(t + 1) * TT].rearrange("(h p) t -> p h t", p=128),
                   wr=[b_mqs[t % 2]])

        load_mq(0)
        NPB = 3
        pT = [sb("pTm%d" % i, [128, 2, TT], BF16) for i in range(NPB)]
        b_pT = bufs(NPB)

        def b1_front(t, h, it):
            mq, b_mq = mqs[t % 2], b_mqs[t % 2]
            pt, b_pt = pT[it % NPB], b_pT[it % NPB]
            for mt in range(2):
                p, bp = next_ps()
                kb.op("pe", lambda e, mt=mt, p=p: e.matmul(
                    p[:], mkT[:, h, mt * 128:(mt + 1) * 128], mq[:, h, :], start=True, stop=True),
                    rd=[b_mkT, b_mq], wr=[bp])
                kb.op("act", lambda e, mt=mt, p=p: e.activation(
                    pt[:, mt, :], p[:], AF.Exp, scale=float(128 ** -0.5)), rd=[bp], wr=[b_pt])

        def b1_back(t, h, it):
            pt, b_pt = pT[it % NPB], b_pT[it % NPB]
            po, bpo = next_ps()
            pd, bpd = next_ps()
            for mt in range(2):
                kb.op("pe", lambda e, mt=mt: e.matmul(
                    po[:], mv[:, mt, h * 128:(h + 1) * 128], pt[:, mt, :], start=(mt == 0), stop=(mt == 1)),
                    rd=[b_mv, b_pt], wr=[bpo])
            for mt in range(2):
                kb.op("pe", lambda e, mt=mt: e.matmul(
                    pd[:], onesb[:], pt[:, mt, :], start=(mt == 0), stop=(mt == 1)),
                    rd=[b_ones, b_pt], wr=[bpd])
            rd_, b_rd = rdn[it % 2], b_rdn[it % 2]
            o_, b_o = mo[it % 2], b_mo[it % 2]
            kb.op("dve", lambda e: e.reciprocal(rd_[:], pd[:]), rd=[bpd], wr=[b_rd])
            kb.op("dve", lambda e: e.tensor_tensor(o_[:], po[:], rd_[:], ALU.mult),
                  rd=[bpo, b_rd], wr=[b_o])
            kb.dma("sp", brT[768 + h * 128:768 + (h + 1) * 128, t * TT:(t + 1) * TT], o_[:], rd=[b_o])

        work = [(t, h) for t in range(NT) for h in range(4)]
        prev = None
        for it, (t, h) in enumerate(work):
            if h == 0 and t + 1 < NT:
                load_mq(t + 1)
            b1_front(t, h, it)
            if prev is not None:
                b1_back(*prev)
            prev = (t, h, it)
        b1_back(*prev)
        phase_end()

    if "2" in phases:
        SBK = 2048
        NSB = L // SBK
        maskf = sb("maskf", [128, 512], F32)
        maskb = sb("maskb", [128, 512], BF16)
        b_maskf, b_mask = Buf(), Buf()
        kb.dma("sp", maskf[:], maskcp_d[:, :], wr=[b_maskf])
        kb.op("dve", lambda e: e.tensor_copy(maskb[:], maskf[:]), rd=[b_maskf], wr=[b_mask])
        selff = sb("selff", [128, 64], F32)
        b_self = Buf()
        kb.dma("sp", selff[:], self_d[:, :], wr=[b_self])
        NSET = 3
        QT = [sb("QT%d" % i, [64, SBK], BF16) for i in range(NSET)]
        KT = [sb("KT%d" % i, [64, 2 * SBK], BF16) for i in range(NSET)]
        VT = [sb("VT%d" % i, [64, 2 * SBK], BF16) for i in range(NSET)]
        b_QT, b_KT, b_VT = bufs(NSET), bufs(NSET), bufs(NSET)
        Vaug = [sb("Vaug%d" % i, [128, 32, 128], BF16) for i in range(2)]
        b_Vaug = bufs(2)
        for i in range(2):
            kb.op("pool", lambda e, i=i: e.memset(Vaug[i][:], 1.0), wr=[b_Vaug[i]])
        NPT = 5
        pi_ = [0]
        PTs = [sb("PTa%d" % i, [128, 512], BF16) for i in range(NPT)]
        PTm = [sb("PTb%d" % i, [128, 512], BF16) for i in range(NPT)]
        b_PTs, b_PTm = bufs(NPT), bufs(NPT)
        accs = [sb("acc%d" % i, [128, SBK], F32) for i in range(2)]
        b_accs = bufs(2)
        oTs = [sb("oTs%d" % i, [64, SBK], BF16) for i in range(2)]
        b_oTs = bufs(2)
        DIL = (1, 4, 16)

        class Head:
            pass

        heads = []
        for sbi in range(NSB):
            for j in range(4):
                for g in range(3):
                    h_ = Head()
                    h_.sbi, h_.j, h_.g, h_.d = sbi, j, g, DIL[g]
                    h_.hh = 4 * g + j
                    h_.t0 = sbi * SBK
                    h_.lbb = 1 if sbi > 0 else 0
                    h_.LB = 128 * h_.d * h_.lbb
                    h_.nper = 16 // h_.d
                    h_.idx = len(heads)
                    h_.slot = sbi * 4 + j
                    heads.append(h_)

        def kidx(h_, r, n):
            return r * (h_.nper + h_.lbb) + (n + h_.lbb)

        def kcol(h_, r, n):
            return h_.d * 128 * (n + h_.lbb) + r

        def loads(h_):
            i = h_.idx % NSET
            hh, t0, LB = h_.hh, h_.t0, h_.LB
            kb.dma("sp", QT[i][:, :], zT[512 + hh * 64:512 + (hh + 1) * 64, t0:t0 + SBK], wr=[b_QT[i]])
            kb.dma("sp", KT[i][:, 0:LB + SBK], zT[1280 + hh * 64:1280 + (hh + 1) * 64, t0 - LB:t0 + SBK], wr=[b_KT[i]])
            kb.dma("sp", VT[i][:, 0:LB + SBK], zT[2048 + hh * 64:2048 + (hh + 1) * 64, t0 - LB:t0 + SBK], wr=[b_VT[i]])

        def vaug_build(h_):
            i = h_.idx % NSET
            vt, bv = VT[i], b_VT[i]
            va, bva = Vaug[h_.idx % 2], b_Vaug[h_.idx % 2]
            d = h_.d
            blks = [(r, n) for r in range(d) for n in range(-h_.lbb, h_.nper)]
            for c0 in range(0, len(blks), 8):
                grp = blks[c0:c0 + 8]
                p, bp = next_ps()
                pb = p.bitcast(BF16)
                for ii, (r, n) in enumerate(grp):
                    cb = kcol(h_, r, n)
                    kb.op("pe", lambda e, ii=ii, cb=cb: e.transpose(
                        pb[:, ii * 64:(ii + 1) * 64], vt[:, cb:cb + 127 * d + 1:d], identb[0:64, 0:64]),
                        rd=[bv, b_identb], wr=[bp])
                i0_ = kidx(h_, *grp[0])
                ng = len(grp)
                kb.op("dve", lambda e: e.tensor_copy(
                    va[:, i0_:i0_ + ng, 0:64], pb[:, 0:ng * 64].rearrange("p (n e) -> p n e", e=64)),
                    rd=[bp], wr=[bva])

        def front(h_, o0, half):
            i = h_.idx % NSET
            qt, kt_, bq, bk = QT[i], KT[i], b_QT[i], b_KT[i]
            d = h_.d
            qbs = [(r, n) for r in range(d) for n in range(h_.nper)]
            two = qbs[o0 + 2 * half:o0 + 2 * half + 2]
            pairs = []
            for qi_, (r, n) in enumerate(two):
                pairs.append((2 * half + qi_, r, n, n, 0))
                if n - 1 >= -h_.lbb:
                    pairs.append((2 * half + qi_, r, n, n - 1, 1))
            p, bp = next_ps()
            for si, (qs, r, nq, nk, mt_) in enumerate(pairs):
                qc = d * 128 * nq + r
                kc = kcol(h_, r, nk)
                kb.op("pe", lambda e, si=si, qc=qc, kc=kc: e.matmul(
                    p[:, si * 128:(si + 1) * 128], kt_[:, kc:kc + 127 * d + 1:d], qt[:, qc:qc + 127 * d + 1:d],
                    start=True, stop=True), rd=[bk, bq], wr=[bp])
            npair = len(pairs)
            k_ = pi_[0]
            pi_[0] += 1
            pa, bpa = PTs[k_ % NPT], b_PTs[k_ % NPT]
            pm, bpm = PTm[k_ % NPT], b_PTm[k_ % NPT]
            kb.op("act", lambda e: e.activation(
                pa[:, 0:128 * npair], p[:, 0:128 * npair], AF.Exp, scale=0.125),
                rd=[bp], wr=[bpa])
            meng = "pool" if k_ % 3 == 0 else "dve"
            if [x_[4] for x_ in pairs] == [0, 1, 0, 1]:
                kb.op(meng, lambda e: e.tensor_tensor(
                    pm[:], pa[:], maskb[:], ALU.mult), rd=[bpa, b_mask], wr=[bpm])
            else:
                for si, (qs, r, nq, nk, mt_) in enumerate(pairs):
                    kb.op(meng, lambda e, si=si, mt_=mt_: e.tensor_tensor(
                        pm[:, si * 128:(si + 1) * 128], pa[:, si * 128:(si + 1) * 128],
                        maskb[:, mt_ * 128:(mt_ + 1) * 128], ALU.mult),
                        rd=[bpa, b_mask], wr=[bpm])
            return pairs, pm, bpm

        obank = {}

        def back(h_, o0, half, pairs, pm, bpm):
            va, bva = Vaug[h_.idx % 2], b_Vaug[h_.idx % 2]
            acc, b_acc = accs[h_.slot % 2], b_accs[h_.slot % 2]
            d = h_.d
            if half == 0:
                obank[(h_.idx, o0)] = next_ps()
            po, bpo = obank[(h_.idx, o0)]
            byq = {}
            for si, pr in enumerate(pairs):
                byq.setdefault(pr[0], []).append((si, pr))
            for qs, lst in byq.items():
                for li, (si, (qs_, r, nq, nk, mt_)) in enumerate(lst):
                    vi = kidx(h_, r, nk)
                    kb.op("pe", lambda e, si=si, vi=vi, qs=qs, li=li, nl=len(lst): e.matmul(
                        po[:, qs * 128:(qs + 1) * 128], va[:, vi, :], pm[:, si * 128:(si + 1) * 128],
                        start=(li == 0), stop=(li == nl - 1)), rd=[bva, bpm], wr=[bpo])
            if half == 1:
                if d == 1:
                    av = acc[:, o0 * 128:(o0 + 4) * 128]
                    pv = po[:, :]
                elif d == 4:
                    r = o0 // 4
                    av = acc[:, r:SBK:4]
                    pv = po[:, :]
                else:
                    av = acc[:, :].rearrange("p (i r) -> p r i", r=16)[:, o0:o0 + 4, :]
                    pv = po[:, :].rearrange("p (r i) -> p r i", i=128)
                if h_.g == 0:
                    kb.op("dve", lambda e: e.tensor_copy(av, pv), rd=[bpo], wr=[b_acc])
                else:
                    kb.op("dve", lambda e: e.tensor_tensor(av, pv, av, ALU.add),
                          rd=[bpo, b_acc], wr=[b_acc])

        def norm_closures(h_):
            acc, b_acc = accs[h_.slot % 2], b_accs[h_.slot % 2]
            ot_, b_ot = oTs[h_.slot % 2], b_oTs[h_.slot % 2]
            j, t0 = h_.j, h_.t0
            cl = []
            for c in range(SBK // 512):
                def f(c=c):
                    cs = slice(c * 512, (c + 1) * 512)
                    kb.op("dve", lambda e: e.reciprocal(acc[64:128, cs], acc[64:128, cs]), rd=[b_acc], wr=[b_acc])
                    p, bp = next_ps()
                    kb.op("pe", lambda e: e.matmul(p[0:64, :], selff[:, :], acc[:, cs], start=True, stop=True),
                          rd=[b_self, b_acc], wr=[bp])
                    kb.op("dve", lambda e: e.tensor_tensor(ot_[:, cs], p[0:64, :], acc[0:64, cs], ALU.mult),
                          rd=[bp, b_acc], wr=[b_ot])
                    if c == SBK // 512 - 1:
                        kb.dma("sp", brT[512 + j * 64:512 + (j + 1) * 64, t0:t0 + SBK], ot_[:, :], rd=[b_ot])
                cl.append(f)
            return cl

        stage_list = [(o0, half) for o0 in range(0, 16, 4) for half in range(2)]
        loads(heads[0])
        if len(heads) > 1:
            loads(heads[1])
        vaug_build(heads[0])
        deferred = []
        for hi_, h_ in enumerate(heads):
            if hi_ + 2 < len(heads):
                loads(heads[hi_ + 2])
            pendq = []
            for si_, (o0, half) in enumerate(stage_list):
                pendq.append((h_, o0, half) + front(h_, o0, half))
                if len(pendq) > 2:
                    back(*pendq.pop(0))
                if si_ == 3 and hi_ + 1 < len(heads):
                    vaug_build(heads[hi_ + 1])
                if deferred and si_ % 2 == 0:
                    deferred.pop(0)()
            while pendq:
                back(*pendq.pop(0))
            while deferred:
                deferred.pop(0)()
            if h_.g == 2:
                deferred = norm_closures(h_)
        while deferred:
            deferred.pop(0)()
        phase_end()

    if "3" in phases:
        TWO_PI = float(2.0 * np.pi)
        MAGIC = 12582912.0
        ctr = [0]

        def tl(shape, dt=F32):
            ctr[0] += 1
            return sb("s3_%d" % ctr[0], shape, dt), Buf()

        def ld(src, shape):
            t_, b_ = tl(shape)
            kb.dma("sp", t_[:], src, wr=[b_])
            return t_, b_

        def flat(a):
            n = len(a.shape)
            if n == 2:
                return a
            if n == 3:
                return a.rearrange("p a b -> p (a b)")
            if n == 4:
                return a.rearrange("p a b c -> p (a b c)")

        def tt(o, bo, a, ba, b, bb_, op, eng="dve"):
            kb.op(eng, lambda e: e.tensor_tensor(o, a, b, op), rd=[ba, bb_], wr=[bo])

        def sincos(ang, b_ang, shape):
            n_, bn = tl(shape)
            sn, bsn = tl(shape)
            cs, bcs = tl(shape)
            r_, br = ang, b_ang
            kb.op("dve", lambda e: e.tensor_scalar(r_, r_, 1.0 / TWO_PI, None, op0=ALU.mult), rd=[br], wr=[br])
            kb.op("dve", lambda e: e.tensor_scalar(n_[:], r_, MAGIC, None, op0=ALU.add), rd=[br], wr=[bn])
            kb.op("dve", lambda e: e.tensor_scalar(n_[:], n_[:], MAGIC, None, op0=ALU.subtract), rd=[bn], wr=[bn])
            tt(n_[:], bn, r_, br, n_[:], bn, ALU.subtract)
            kb.op("act", lambda e: e.activation(sn[:], n_[:], AF.Sin, scale=TWO_PI), rd=[bn], wr=[bsn])
            kb.op("dve", lambda e: e.tensor_scalar(r_, r_, 0.25, None, op0=ALU.add), rd=[br], wr=[br])
            kb.op("dve", lambda e: e.tensor_scalar(n_[:], r_, MAGIC, None, op0=ALU.add), rd=[br], wr=[bn])
            kb.op("dve", lambda e: e.tensor_scalar(n_[:], n_[:], MAGIC, None, op0=ALU.subtract), rd=[bn], wr=[bn])
            tt(n_[:], bn, r_, br, n_[:], bn, ALU.subtract)
            kb.op("act", lambda e: e.activation(cs[:], n_[:], AF.Sin, scale=TWO_PI), rd=[bn], wr=[bcs])
            return cs, bcs, sn, bsn, n_, bn

        pm3 = (nc.sbuf_base, nc.sbuf_top)
        WvM = [sb("WvM%d" % m_, [128, 4, 8, 128], BF16) for m_ in range(4)]
        b_WvM = bufs(4)
        Kc = sb("Kc", [128, 4, 8, 128], BF16); b_Kc = Buf()
        WcT = sb("WcT", [128, 32, 8, 64], BF16); b_WcT = Buf()
        NLV = 7
        TbbP = sb("TbbP", [128, 4, 128], F32); b_TbbP = Buf()
        wglu = sb("wglu", [128, 4, 512], BF16); b_wglu = Buf()
        kb.dma("pool", wglu[:], w_glu.rearrange("(k p) c -> p k c", p=128), wr=[b_wglu], max_dma_last_dim=4096)
        bglu = sb("bglu", [128, 4], F32); b_bglu = Buf()
        kb.dma("sp", bglu[:], bgluT[:, :], wr=[b_bglu])
        pm3b = (nc.sbuf_base, nc.sbuf_top)

        lamr, b_lamr = ld(lamC_re_d[:, :, :], [128, 4, 64])
        lami, b_lami = ld(lamC_im_d[:, :, :], [128, 4, 64])
        dtc, b_dtc = ld(dtC_d[:, :], [128, 4])
        bre, b_bre = ld(bC_re_d[:, :, :], [128, 4, 64])
        bim, b_bim = ld(bC_im_d[:, :, :], [128, 4, 64])
        k7, b_k7 = ld(k7_d[:, :], [128, 8])
        eo, b_eo = ld(eo_d[:, :], [128, 4])

        kb.op("act", lambda e: e.activation(dtc[:], dtc[:], AF.Exp), rd=[b_dtc], wr=[b_dtc])
        LRD, b_LRD = tl([128, 4, 64]); LID, b_LID = tl([128, 4, 64])
        dtb = dtc[:, :].unsqueeze(2).to_broadcast([128, 4, 64])
        tt(LRD[:], b_LRD, lamr[:], b_lamr, dtb, b_dtc, ALU.mult)
        tt(LID[:], b_LID, lami[:], b_lami, dtb, b_dtc, ALU.mult)
        mag1, b_mag1 = tl([128, 4, 64])
        kb.op("act", lambda e: e.activation(mag1[:], LRD[:], AF.Exp), rd=[b_LRD], wr=[b_mag1])
        LID2, b_LID2 = tl([128, 4, 64])
        kb.op("dve", lambda e: e.tensor_copy(LID2[:], LID[:]), rd=[b_LID], wr=[b_LID2])
        c1, b_c1, s1, b_s1, _, _ = sincos(LID2[:], b_LID2, [128, 4, 64])
        are, b_are = tl([128, 4, 64]); aim, b_aim = tl([128, 4, 64])
        tt(are[:], b_are, mag1[:], b_mag1, c1[:], b_c1, ALU.mult)
        tt(aim[:], b_aim, mag1[:], b_mag1, s1[:], b_s1, ALU.mult)
        kb.op("dve", lambda e: e.tensor_scalar(are[:], are[:], -1.0, None, op0=ALU.add), rd=[b_are], wr=[b_are])
        den, b_den = tl([128, 4, 64]); t1, b_t1 = tl([128, 4, 64]); t2, b_t2 = tl([128, 4, 64])
        tt(den[:], b_den, lamr[:], b_lamr, lamr[:], b_lamr, ALU.mult)
        tt(t1[:], b_t1, lami[:], b_lami, lami[:], b_lami, ALU.mult)
        tt(den[:], b_den, den[:], b_den, t1[:], b_t1, ALU.add)
        kb.op("dve", lambda e: e.reciprocal(den[:], den[:]), rd=[b_den], wr=[b_den])
        cre, b_cre = tl([128, 4, 64]); cim, b_cim = tl([128, 4, 64])
        tt(t1[:], b_t1, are[:], b_are, lamr[:], b_lamr, ALU.mult)
        tt(t2[:], b_t2, aim[:], b_aim, lami[:], b_lami, ALU.mult)
        tt(t1[:], b_t1, t1[:], b_t1, t2[:], b_t2, ALU.add)
        tt(cre[:], b_cre, t1[:], b_t1, den[:], b_den, ALU.mult)
        tt(t1[:], b_t1, aim[:], b_aim, lamr[:], b_lamr, ALU.mult)
        tt(t2[:], b_t2, are[:], b_are, lami[:], b_lami, ALU.mult)
        tt(t1[:], b_t1, t1[:], b_t1, t2[:], b_t2, ALU.subtract)
        tt(cim[:], b_cim, t1[:], b_t1, den[:], b_den, ALU.mult)
        bbcat, b_bbcat = tl([128, 4, 128])
        tt(t1[:], b_t1, cre[:], b_cre, bre[:], b_bre, ALU.mult)
        tt(t2[:], b_t2, cim[:], b_cim, bim[:], b_bim, ALU.mult)
        tt(bbcat[:, :, 0:64], b_bbcat, t1[:], b_t1, t2[:], b_t2, ALU.subtract)
        tt(t1[:], b_t1, cre[:], b_cre, bim[:], b_bim, ALU.mult)
        tt(t2[:], b_t2, cim[:], b_cim, bre[:], b_bre, ALU.mult)
        tt(bbcat[:, :, 64:128], b_bbcat, t1[:], b_t1, t2[:], b_t2, ALU.add)
        SH = [128, 4, 8, 64]
        argR, b_argR = tl(SH); argI, b_argI = tl(SH)
        k7b = k7[:, :].unsqueeze(1).unsqueeze(3).to_broadcast(SH)
        tt(argR[:], b_argR, LRD[:, :, :].unsqueeze(2).to_broadcast(SH), b_LRD, k7b, b_k7, ALU.mult)
        tt(argI[:], b_argI, LID[:, :, :].unsqueeze(2).to_broadcast(SH), b_LID, k7b, b_k7, ALU.mult)
        kb.op("act", lambda e: e.activation(flat(argR[:]), flat(argR[:]), AF.Exp), rd=[b_argR], wr=[b_argR])
        ck, b_ck, sk, b_sk, nsc, b_nsc = sincos(flat(argI[:]), b_argI, [128, 2048])
        pre, b_pre = ck, b_ck
        pim, b_pim = sk, b_sk
        tt(pre[:], b_pre, flat(argR[:]), b_argR, ck[:], b_ck, ALU.mult)
        tt(pim[:], b_pim, flat(argR[:]), b_argR, sk[:], b_sk, ALU.mult)
        pre4 = pre[:, :].rearrange("p (a b c) -> p a b c", a=4, b=8)
        pim4 = pim[:, :].rearrange("p (a b c) -> p a b c", a=4, b=8)
        bbr_b = bbcat[:, :, 0:64].unsqueeze(2).to_broadcast(SH)
        bbi_b = bbcat[:, :, 64:128].unsqueeze(2).to_broadcast(SH)
        wvr, b_wvr = argR, b_argR
        wvi, b_wvi = argI, b_argI
        t4, b_t4 = nsc[:, :].rearrange("p (a b c) -> p a b c", a=4, b=8), b_nsc
        tt(wvr[:], b_wvr, pre4, b_pre, bbr_b, b_bbcat, ALU.mult)
        tt(t4[:], b_t4, pim4, b_pim, bbi_b, b_bbcat, ALU.mult)
        tt(wvr[:], b_wvr, wvr[:], b_wvr, t4[:], b_t4, ALU.subtract)
        tt(wvi[:], b_wvi, pre4, b_pre, bbi_b, b_bbcat, ALU.mult)
        tt(t4[:], b_t4, pim4, b_pim, bbr_b, b_bbcat, ALU.mult)
        tt(wvi[:], b_wvi, wvi[:], b_wvi, t4[:], b_t4, ALU.add)
        for (W_, bW, col) in [(WvM[m_], b_WvM[m_], m_) for m_ in range(4)]:
            kb.op("dve", lambda e, W_=W_, col=col: e.tensor_scalar(W_[:, :, :, 0:64], wvr[:], eo[:, col:col + 1], None, op0=ALU.mult),
                  rd=[b_wvr, b_eo], wr=[bW])
            kb.op("dve", lambda e, W_=W_, col=col: e.tensor_scalar(W_[:, :, :, 64:128], wvi[:], eo[:, col:col + 1], None, op0=ALU.mult),
                  rd=[b_wvi, b_eo], wr=[bW])
        for gq in range(4):
            p, bp = next_ps()
            kb.op("pe", lambda e, gq=gq, p=p: e.transpose(p[:, 0:128], bbcat[:, gq, :], identf[:]),
                  rd=[b_bbcat, b_identf], wr=[bp])
            kb.op("dve", lambda e, gq=gq, p=p: e.tensor_copy(TbbP[:, gq, :], p[:, 0:128]), rd=[bp], wr=[b_TbbP])

        kb.barrier()
        nc.sbuf_base, nc.sbuf_top = pm3b
        Rot = sb("Rot", [128, NLV, 32, 128], BF16); b_Rot = Buf()
        pm3b = (nc.sbuf_base, nc.sbuf_top)
        k9, b_k9 = ld(k9_d[:, :], [128, 9])
        bmask, b_bmask = ld(bmask_d[:, :], [128, 8])
        pswap, b_pswap = ld(pswap_d[:, :], [128, 128])
        sgn, b_sgn = ld(sgn_d[:, :], [128, 1])
        dcol, b_dcol = ld(dcolC_d[:, :], [128, 4])
        lpr, b_lpr = ld(lamP_re_d[:, :], [128, 32]); lpi, b_lpi = ld(lamP_im_d[:, :], [128, 32])
        dtp, b_dtp = ld(dtP_d[:, :], [128, 32])
        cU, b_cU = ld(cU_d[:, :, :], [128, 32, 16]); cW, b_cW = ld(cW_d[:, :, :], [128, 32, 16])
        kb.op("act", lambda e: e.activation(dtp[:], dtp[:], AF.Exp), rd=[b_dtp], wr=[b_dtp])
        tt(lpr[:], b_lpr, lpr[:], b_lpr, dtp[:], b_dtp, ALU.mult)
        tt(lpi[:], b_lpi, lpi[:], b_lpi, dtp[:], b_dtp, ALU.mult)
        S9 = [128, 32, 9]
        a9r, b_a9r = tl(S9); a9i, b_a9i = tl(S9)
        k9b = k9[:, :].unsqueeze(1).to_broadcast(S9)
        tt(a9r[:], b_a9r, lpr[:, :].unsqueeze(2).to_broadcast(S9), b_lpr, k9b, b_k9, ALU.mult)
        tt(a9i[:], b_a9i, lpi[:, :].unsqueeze(2).to_broadcast(S9), b_lpi, k9b, b_k9, ALU.mult)
        kb.op("act", lambda e: e.activation(flat(a9r[:]), flat(a9r[:]), AF.Exp), rd=[b_a9r], wr=[b_a9r])
        c9, b_c9, s9, b_s9, _, _ = sincos(flat(a9i[:]), b_a9i, [128, 288])
        P9r, b_P9r = tl(S9); P9i, b_P9i = tl(S9)
        tt(flat(P9r[:]), b_P9r, flat(a9r[:]), b_a9r, c9[:], b_c9, ALU.mult)
        tt(flat(P9i[:]), b_P9i, flat(a9r[:]), b_a9r, s9[:], b_s9, ALU.mult)
        SE = [128, 32, 9, 16]
        E, b_E = tl(SE); E2, b_E2 = tl(SE)
        tt(E[:], b_E, cU[:, :, :].unsqueeze(2).to_broadcast(SE), b_cU, P9r[:, :, :].unsqueeze(3).to_broadcast(SE), b_P9r, ALU.mult)
        tt(E2[:], b_E2, cW[:, :, :].unsqueeze(2).to_broadcast(SE), b_cW, P9i[:, :, :].unsqueeze(3).to_broadcast(SE), b_P9i, ALU.mult)
        kb.op("dve", lambda e: e.scalar_tensor_tensor(out=flat(E[:]), in0=flat(E[:]), scalar=sgn[:, 0:1], in1=flat(E2[:]),
                                                      op0=ALU.mult, op1=ALU.subtract), rd=[b_E, b_E2, b_sgn], wr=[b_E])
        kb.op("pool", lambda e: e.memset(WcT[:], 0.0), wr=[b_WcT])
        for par in range(4):
            kb.op("dve", lambda e, par=par: e.tensor_copy(WcT[:, par:32:4, :, par * 16:(par + 1) * 16], E[:, par:32:4, 1:9, :]),
                  rd=[b_E], wr=[b_WcT])
        Ar, b_Ar = tl([128, NLV, 32]); Ai, b_Ai = tl([128, NLV, 32]); tq, b_tq = tl([128, 32])
        kb.op("dve", lambda e: e.tensor_copy(Ar[:, 0, :], P9r[:, :, 8]), rd=[b_P9r], wr=[b_Ar])
        kb.op("dve", lambda e: e.tensor_copy(Ai[:, 0, :], P9i[:, :, 8]), rd=[b_P9i], wr=[b_Ai])
        for l in range(1, NLV):
            tt(tq[:], b_tq, Ai[:, l - 1, :], b_Ai, Ai[:, l - 1, :], b_Ai, ALU.mult)
            tt(Ar[:, l, :], b_Ar, Ar[:, l - 1, :], b_Ar, Ar[:, l - 1, :], b_Ar, ALU.mult)
            tt(Ar[:, l, :], b_Ar, Ar[:, l, :], b_Ar, tq[:], b_tq, ALU.subtract)
            tt(tq[:], b_tq, Ar[:, l - 1, :], b_Ar, Ai[:, l - 1, :], b_Ai, ALU.mult)
            kb.op("dve", lambda e, l=l: e.tensor_scalar(Ai[:, l, :], tq[:], 2.0, None, op0=ALU.mult), rd=[b_tq], wr=[b_Ai])
        Bi, b_Bi = tl([128, NLV, 32])
        kb.op("dve", lambda e: e.tensor_scalar(Bi[:], Ai[:], sgn[:, 0:1], None, op0=ALU.mult), rd=[b_Ai, b_sgn], wr=[b_Bi])
        rts = [tl([128, 128]) for _ in range(4)]
        b_RotL = bufs(NLV)
        ri = 0
        for l in range(NLV):
            for g in range(32):
                rt, b_rt = rts[ri % 4]
                ri += 1
                kb.op("dve", lambda e, l=l, g=g, rt=rt: e.tensor_scalar(rt[:], identf[:], Ar[:, l, g:g + 1], None, op0=ALU.mult),
                      rd=[b_identf, b_Ar], wr=[b_rt])
                kb.op("dve", lambda e, l=l, g=g, rt=rt: e.scalar_tensor_tensor(out=Rot[:, l, g, :], in0=pswap[:], scalar=Bi[:, l, g:g + 1],
                                                                                 in1=rt[:], op0=ALU.mult, op1=ALU.add),
                      rd=[b_pswap, b_Bi, b_rt], wr=[b_RotL[l]])
        b_Rot = Buf()
        kb.op("dve", lambda e: e.tensor_copy(Rot[:, 0, 0, 0:1], Rot[:, 0, 0, 0:1]), rd=b_RotL, wr=[b_Rot])
        KtT, b_KtT = tl([128, 128]); KtS, b_KtS = tl([128, 8, 16]); Kc32, b_Kc32 = tl([128, 8, 8, 16])
        for gq in range(4):
            p, bp = next_ps()
            for g8 in range(8):
                g = gq * 8 + g8
                kb.op("pe", lambda e, g=g, g8=g8, gq=gq, p=p: e.matmul(
                    p[:, g8 * 16:(g8 + 1) * 16], E[:, g, 0:8, :].rearrange("p a b -> p (a b)"), TbbP[:, gq, g8 * 16:(g8 + 1) * 16],
                    start=True, stop=True), rd=[b_E, b_TbbP], wr=[bp])
            kb.op("dve", lambda e, p=p: e.tensor_copy(KtT[:], p[:, 0:128]), rd=[bp], wr=[b_KtT])
            p2, bp2 = next_ps()
            kb.op("pe", lambda e, p2=p2: e.transpose(p2[:, 0:128], KtT[:], identf[:]), rd=[b_KtT, b_identf], wr=[bp2])
            kb.op("dve", lambda e, p2=p2: e.tensor_copy(KtS[:].rearrange("p a b -> p (a b)"), p2[:, 0:128]), rd=[bp2], wr=[b_KtS])
            SK = [128, 8, 8, 16]
            tt(Kc32[:], b_Kc32, KtS[:, :, :].unsqueeze(2).to_broadcast(SK), b_KtS,
               bmask[:, :].unsqueeze(1).unsqueeze(3).to_broadcast(SK), b_bmask, ALU.mult)
            kb.op("dve", lambda e, gq=gq: e.scalar_tensor_tensor(
                out=Kc32[:, 0, :, :].rearrange("p a b -> p (a b)"), in0=identf[:], scalar=dcol[:, gq:gq + 1],
                in1=Kc32[:, 0, :, :].rearrange("p a b -> p (a b)"), op0=ALU.mult, op1=ALU.add),
                rd=[b_identf, b_dcol, b_Kc32], wr=[b_Kc32])
            kb.op("dve", lambda e, gq=gq: e.tensor_copy(Kc[:, gq, :, :].rearrange("p a b -> p (a b)"), flat(Kc32[:])),
                  rd=[b_Kc32], wr=[b_Kc])
        kb.barrier()
        nc.sbuf_base, nc.sbuf_top = pm3b

        CH = 1024
        J = CH // 8
        NCH = L // CH
        uTs = [sb("uT%d" % i, [128, 4, CH], BF16) for i in range(2)]
        b_uTs = bufs(2)
        uTd = sb("uTd", [128, 4, 8, J], BF16); b_uTd = bufs(4)
        Xf = sb("Xf", [128, 32, J], F32); b_Xf = bufs(8)
        Xb = sb("Xb", [128, 32, J], BF16); b_Xb = bufs(8)
        Vl = sb("Vl", [128, 32], F32); b_Vl = Buf()
        Sin = sb("Sin", [128, 32], F32); b_Sin = Buf()
        kb.op("dve", lambda e: e.memset(Sin[:], 0.0), wr=[b_Sin])
        ygT = sb("ygT", [128, 4, CH], BF16); b_ygT = bufs(4)
        gt1, b_gt1 = sb("gt1", [128, 512], F32), Buf()
        gt2, b_gt2 = sb("gt2", [128, 512], F32), Buf()
        gsg, b_gsg = sb("gsg", [128, 512], F32), Buf()
        y2s = [sb("y2s%d" % i, [128, 512], BF16) for i in range(2)]
        b_y2s = bufs(2)
        sg2, b_sg2 = sb("sg2", [128, 512], BF16), Buf()

        def load_u(c):
            kb.dma("sp", uTs[c % 2][:], zT[0:512, c * CH:(c + 1) * CH].rearrange("(k p) t -> p k t", p=128), wr=[b_uTs[c % 2]])

        load_u(0)
        yi = 0
        for c in range(NCH):
            if c + 1 < NCH:
                load_u(c + 1)
            uT, b_uT = uTs[c % 2], b_uTs[c % 2]
            for ct in range(4):
                kb.op("pool" if ct % 2 == 0 else "dve", lambda e, ct=ct: e.tensor_copy(
                    uTd[:, ct, :, :], uT[:, ct, :].rearrange("p (j t) -> p t j", t=8)),
                    rd=[b_uT], wr=[b_uTd[ct]])
            for gb in range(8):
                p, bp = next_ps()
                for gi in range(4):
                    g = gb * 4 + gi
                    gq, g8 = g // 8, g % 8
                    r0 = 64 * (g8 // 4)
                    W_, bW = WvM[g8 % 4], b_WvM[g8 % 4]
                    for t0 in range(8):
                        kb.op("pe", lambda e, gi=gi, gq=gq, r0=r0, W_=W_, t0=t0, p=p, uT=uT: e.matmul(
                            p[:, gi * J:(gi + 1) * J], W_[r0:r0 + 64, gq, t0, :], uTd[r0:r0 + 64, gq, t0, :],
                            start=(t0 == 0), stop=(t0 == 7)), rd=[bW, b_uTd[gq]], wr=[bp])
                pv = p[:, :].rearrange("p (g j) -> p g j", j=J)
                kb.op("dve", lambda e, gb=gb, pv=pv: e.tensor_copy(Xf[:, gb * 4:(gb + 1) * 4, 1:J], pv[:, :, 0:J - 1]),
                      rd=[bp], wr=[b_Xf[gb]])
                kb.op("dve", lambda e, gb=gb, pv=pv: e.tensor_copy(Vl[:, gb * 4:(gb + 1) * 4], pv[:, :, J - 1]),
                      rd=[bp], wr=[b_Vl])
                kb.op("dve", lambda e, gb=gb: e.tensor_copy(Xf[:, gb * 4:(gb + 1) * 4, 0], Sin[:, gb * 4:(gb + 1) * 4]),
                      rd=[b_Sin], wr=[b_Xf[gb]])
                kb.op("act", lambda e, gb=gb: e.copy(Xb[:, gb * 4:(gb + 1) * 4, :], Xf[:, gb * 4:(gb + 1) * 4, :]),
                      rd=[b_Xf[gb]], wr=[b_Xb[gb]])
            for l in range(NLV):
                s_ = 1 << l
                for gb in range(8):
                    p, bp = next_ps()
                    for gi in range(4):
                        g = gb * 4 + gi
                        kb.op("pe", lambda e, gi=gi, g=g, l=l, s_=s_, p=p: e.matmul(
                            p[:, gi * J + s_:(gi + 1) * J], Rot[:, l, g, :], Xb[:, g, 0:J - s_], start=True, stop=True),
                            rd=[b_Rot, b_Xb[gb]], wr=[bp])
                    pv = p[:, :].rearrange("p (g j) -> p g j", j=J)
                    kb.op("dve", lambda e, gb=gb, pv=pv, s_=s_: e.tensor_tensor(
                        Xf[:, gb * 4:(gb + 1) * 4, s_:J], pv[:, :, s_:J], Xf[:, gb * 4:(gb + 1) * 4, s_:J], ALU.add),
                        rd=[bp, b_Xf[gb]], wr=[b_Xf[gb]])
                    kb.op("act", lambda e, gb=gb: e.copy(Xb[:, gb * 4:(gb + 1) * 4, :], Xf[:, gb * 4:(gb + 1) * 4, :]),
                          rd=[b_Xf[gb]], wr=[b_Xb[gb]])
            p, bp = next_ps()
            for g in range(32):
                kb.op("pe", lambda e, g=g, p=p: e.matmul(p[:, g:g + 1], Rot[:, 0, g, :], Xb[:, g, J - 1:J], start=True, stop=True),
                      rd=[b_Rot, b_Xb[g // 4]], wr=[bp])
            kb.op("dve", lambda e, p=p: e.tensor_tensor(Sin[:], p[:, 0:32], Vl[:], ALU.add), rd=[bp, b_Vl], wr=[b_Sin])
            for ct in range(4):
                pbk = [next_ps() for _ in range(2)]
                pv_ = [pbk[b_][0][:, :].rearrange("p (t j) -> p t j", j=J) for b_ in range(2)]
                firsts = [True, True]
                for tau in range(8):
                    for t0 in range(tau, 8):
                        bi_ = t0 // 4
                        kb.op("pe", lambda e, tau=tau, t0=t0, bi_=bi_, st=firsts[bi_]: e.matmul(
                            pv_[bi_][:, t0 % 4, :], Kc[:, ct, tau, :], uTd[:, ct, t0 - tau, :],
                            start=st, stop=False), rd=[b_Kc, b_uTd[ct]], wr=[pbk[bi_][1]])
                        firsts[bi_] = False
                for g8 in range(8):
                    g = ct * 8 + g8
                    r0 = 64 * (g8 // 4)
                    for t0 in range(8):
                        bi_ = t0 // 4
                        last = (g8 % 4 == 3 and t0 % 4 == 3)
                        kb.op("pe", lambda e, g=g, r0=r0, t0=t0, bi_=bi_, last=last: e.matmul(
                            pv_[bi_][r0:r0 + 64, t0 % 4, :], WcT[:, g, t0, :], Xb[:, g, :],
                            start=False, stop=last), rd=[b_WcT, b_Xb[g // 4]], wr=[pbk[bi_][1]])
                for bi_ in range(2):
                    p, bp = pbk[bi_]
                    kb.op("act", lambda e, p=p: e.activation(gt1[:], p[:], AF.Square), rd=[bp], wr=[b_gt1])
                    kb.op("dve", lambda e: e.tensor_scalar(gt1[:], gt1[:], 0.044715, 1.0, op0=ALU.mult, op1=ALU.add),
                          rd=[b_gt1], wr=[b_gt1])
                    kb.op("dve", lambda e, p=p: e.tensor_tensor(gt2[:], p[:], gt1[:], ALU.mult), rd=[bp, b_gt1], wr=[b_gt2])
                    kb.op("act", lambda e: e.activation(gsg[:], gt2[:], AF.Sigmoid, scale=1.5957691216057308),
                          rd=[b_gt2], wr=[b_gsg])
                    yo = ygT[:, ct, :].rearrange("p (j t) -> p t j", t=8)[:, 4 * bi_:4 * bi_ + 4, :]
                    kb.op("dve", lambda e, bi_=bi_, yo=yo: e.tensor_tensor(
                        yo, pv_[bi_], gsg[:, :].rearrange("p (t j) -> p t j", j=J), ALU.mult),
                        rd=[bp, b_gsg], wr=[b_ygT[ct]])
            for hf in range(CH // 512):
                osl = slice(hf * 512, (hf + 1) * 512)
                for ct in range(4):
                    p, bp = next_ps()
                    for k in range(4):
                        kb.op("pe", lambda e, k=k, ct=ct, p=p, osl=osl: e.matmul(
                            p[:], wglu[:, k, ct * 128:(ct + 1) * 128], ygT[:, k, osl], start=(k == 0), stop=(k == 3)),
                            rd=[b_wglu, b_ygT[k]], wr=[bp])
                    kb.op("act", lambda e, p=p, ct=ct: e.activation(sg2[:], p[:], AF.Sigmoid, bias=bglu[:, ct:ct + 1], scale=1.0),
                          rd=[bp, b_bglu], wr=[b_sg2])
                    y2, by2 = y2s[yi % 2], b_y2s[yi % 2]
                    yi += 1
                    kb.op("dve", lambda e, y2=y2, ct=ct, osl=osl: e.tensor_tensor(y2[:], ygT[:, ct, osl], sg2[:], ALU.mult),
                          rd=[b_ygT[ct], b_sg2], wr=[by2])
                    kb.dma("sp", brT[ct * 128:(ct + 1) * 128, c * CH + hf * 512:c * CH + (hf + 1) * 512], y2[:], rd=[by2])
        phase_end()

    if "4" in phases:
        def ldw(name, src, kt_n, ncol, kp=128):
            w = sb(name, [kp, kt_n, ncol], BF16)
            bw = Buf()
            kb.dma("pool", w[:], src.rearrange("(k p) c -> p k c", p=kp), wr=[bw], max_dma_last_dim=4096)
            return w, bw
        wssm, b_wssm = ldw("wssm", w_ssm_br, 4, D)
        wattn, b_wattn = ldw("wattn", w_attn_br, 4, D, kp=64)
        wmem, b_wmem = ldw("wmem", w_mem_br, 4, D)
        wo, b_wo = ldw("wo", w_o, 8, D)
        brs = [sb("brs%d" % i, [128, 8, TT], BF16) for i in range(2)]
        b_brs = bufs(2)
        gts = [sb("gts%d" % i, [128, 24, TT], BF16) for i in range(2)]
        b_gts = bufs(2)
        xts4 = [sb("xt4%d" % i, [128, 4, D], F32) for i in range(2)]
        b_xts4 = bufs(2)
        mg = sb("mg", [128, 8, TT], BF16)
        b_mg = bufs(8)
        mm = [sb("mm%d" % i, [128, TT], F32) for i in range(6)]
        b_mm = bufs(6)

        def load4(t):
            i = t % 2
            sl = slice(t * TT, (t + 1) * TT)
            kb.dma("sp", brs[i][:, 0:4, :], brT[0:512, sl].rearrange("(k p) t -> p k t", p=128), wr=[b_brs[i]])
            kb.dma("sp", brs[i][0:64, 4:8, :], brT[512:768, sl].rearrange("(k p) t -> p k t", p=64), wr=[b_brs[i]])
            kb.dma("sp", gts[i][:], zT[3328:6400, sl].rearrange("(k p) t -> p k t", p=128), wr=[b_gts[i]])
            kb.dma("sp", xts4[i][:], x[sl, :].rearrange("(b p) c -> p b c", p=128), wr=[b_xts4[i]])
        mms = [sb("mms%d" % i, [128, 4, TT], BF16) for i in range(2)]
        b_mms = bufs(2)

        def load4b(t):
            i = t % 2
            sl = slice(t * TT, (t + 1) * TT)
            kb.dma("sp", mms[i][:], brT[768:1280, sl].rearrange("(k p) t -> p k t", p=128), wr=[b_mms[i]])

        load4(0)
        load4b(0)
        for t in range(NT):
            if t + 1 < NT:
                load4(t + 1)
                load4b(t + 1)
            i = t % 2
            br_, bbr = brs[i], b_brs[i]
            gt, bgt = gts[i], b_gts[i]
            xt, bxt = xts4[i], b_xts4[i]
            mt_, bmt = mms[i], b_mms[i]
            for ct in range(8):
                cs = slice(ct * 128, (ct + 1) * 128)
                p0, bp0 = next_ps()
                for k in range(4):
                    kb.op("pe", lambda e, k=k, cs=cs, p0=p0, br_=br_: e.matmul(
                        p0[:], wssm[:, k, cs], br_[:, k, :], start=(k == 0), stop=(k == 3)),
                        rd=[b_wssm, bbr], wr=[bp0])
                p1, bp1 = next_ps()
                for k in range(4):
                    kb.op("pe", lambda e, k=k, cs=cs, p1=p1, br_=br_: e.matmul(
                        p1[:], wattn[:, k, cs], br_[0:64, 4 + k, :], start=(k == 0), stop=(k == 3)),
                        rd=[b_wattn, bbr], wr=[bp1])
                p2, bp2 = next_ps()
                for k in range(4):
                    kb.op("pe", lambda e, k=k, cs=cs, p2=p2, mt_=mt_: e.matmul(
                        p2[:], wmem[:, k, cs], mt_[:, k, :], start=(k == 0), stop=(k == 3)),
                        rd=[b_wmem, bmt], wr=[bp2])
                a0, a1, a2 = mm[(3 * ct) % 6], mm[(3 * ct + 1) % 6], mm[(3 * ct + 2) % 6]
                ba0, ba1, ba2 = b_mm[(3 * ct) % 6], b_mm[(3 * ct + 1) % 6], b_mm[(3 * ct + 2) % 6]
                kb.op("dve", lambda e, a0=a0, p0=p0, gt=gt, ct=ct: e.tensor_tensor(a0[:], p0[:], gt[:, ct, :], ALU.mult),
                      rd=[bp0, bgt], wr=[ba0])
                kb.op("dve", lambda e, a1=a1, p1=p1, gt=gt, ct=ct: e.tensor_tensor(a1[:], p1[:], gt[:, 8 + ct, :], ALU.mult),
                      rd=[bp1, bgt], wr=[ba1])
                kb.op("dve", lambda e, a2=a2, p2=p2, gt=gt, ct=ct: e.tensor_tensor(a2[:], p2[:], gt[:, 16 + ct, :], ALU.mult),
                      rd=[bp2, bgt], wr=[ba2])
                kb.op("pool", lambda e, a0=a0, a1=a1: e.tensor_tensor(a0[:], a0[:], a1[:], ALU.add),
                      rd=[ba0, ba1], wr=[ba0])
                kb.op("pool", lambda e, a0=a0, a2=a2, ct=ct: e.tensor_tensor(mg[:, ct, :], a0[:], a2[:], ALU.add),
                      rd=[ba0, ba2], wr=[b_mg[ct]])
            for tb in range(4):
                for ch in range(2):
                    p, bp = next_ps()
                    for k in range(8):
                        kb.op("pe", lambda e, k=k, tb=tb, ch=ch, p=p: e.matmul(
                            p[:], mg[:, k, tb * 128:(tb + 1) * 128], wo[:, k, ch * 512:(ch + 1) * 512],
                            start=(k == 0), stop=(k == 7)), rd=[b_mg[k], b_wo], wr=[bp])
                    kb.op("dve", lambda e, tb=tb, ch=ch, p=p, xt=xt: e.tensor_tensor(
                        xt[:, tb, ch * 512:(ch + 1) * 512], p[:], xt[:, tb, ch * 512:(ch + 1) * 512], ALU.add),
                        rd=[bp, bxt], wr=[bxt])
            kb.dma("sp", hscr[t * TT:(t + 1) * TT, :].rearrange("(b p) c -> p b c", p=128), xt[:], rd=[bxt])
        phase_end()

    if "C" in phases:
        TC = 256
        NBC = TC // 128
        NTC = L // TC
        src = hscr if "4" in phases else x
        wup = sb("wup", [128, 8, DFF], BF16)
        b_wup = bufs(8)
        wdn = sb("wdn", [128, 32, D], BF16)
        b_wdn = bufs(32)
        b_wupc = bufs(8)
        for cc in range(8):
            kb.dma("pool", wup[:, :, cc * 512:(cc + 1) * 512],
                   w_up[:, cc * 512:(cc + 1) * 512].rearrange("(k p) c -> p k c", p=128), wr=[b_wupc[cc]],
                   max_dma_last_dim=2048)
        for f in range(32):
            kb.dma("pool", wdn[:, f, :], w_down[f * 128:(f + 1) * 128, :], wr=[b_wdn[f]],
                   max_dma_last_dim=4096)
        g2 = sb("g2", [128, D], F32)
        gf = sb("gf", [128, D], F32)
        b_g2, b_gf = Buf(), Buf()
        kb.dma("sp", g2[:], g2rep[:, :], wr=[b_g2])
        kb.dma("sp", gf[:], gfrep[:, :], wr=[b_gf])
        hts = [sb("htC%d" % i, [128, NBC, D], F32) for i in range(3)]
        b_hts = bufs(3)
        nbt2s = [sb("nbtC%d" % i, [128, NBC, D], BF16) for i in range(2)]
        b_nbt2s = bufs(2)
        n2Ts = [sb("n2T%d" % i, [128, 8, TC], BF16) for i in range(2)]
        b_n2Ts = [bufs(8) for _ in range(2)]
        hid = sb("hid", [128, 32, TC], BF16)
        b_hid = bufs(32)
        rl = [sb("rl%d" % i, [128, TC], F32) for i in range(2)]
        b_rl = bufs(2)

        def load_h(t):
            kb.dma("sp", hts[t % 3][:], src[t * TC:(t + 1) * TC, :].rearrange("(b p) c -> p b c", p=128),
                   wr=[b_hts[t % 3]])

        def normC(t):
            norm_to_T(hts[t % 3], b_hts[t % 3], g2, b_g2, nbt2s[t % 2], b_nbt2s[t % 2], n2Ts[t % 2], b_n2Ts[t % 2], nblk=NBC)

        load_h(0)
        if NTC > 1:
            load_h(1)
        normC(0)
        for t in range(NTC):
            ht, b_ht = hts[t % 3], b_hts[t % 3]
            n2T, b_n2T = n2Ts[t % 2], b_n2Ts[t % 2]
            for f in range(32):
                if f == 6 and t + 1 < NTC:
                    normC(t + 1)
                if f == 20 and t + 2 < NTC:
                    load_h(t + 2)
                p, bp = next_ps()
                for kt in range(8):
                    kb.op("pe", lambda e, kt=kt, f=f, p=p: e.matmul(
                        p[:, 0:TC], wup[:, kt, f * 128:(f + 1) * 128], n2T[:, kt, :],
                        start=(kt == 0), stop=(kt == 7)),
                        rd=[b_wupc[f // 4], b_n2T[kt]], wr=[bp])
                r, br = rl[f % 2], b_rl[f % 2]
                kb.op("act", lambda e, r=r, p=p: e.activation(r[:], p[:, 0:TC], AF.Relu), rd=[bp], wr=[br])
                kb.op("pool", lambda e, r=r, f=f: e.tensor_tensor(hid[:, f, :], r[:], r[:], ALU.mult),
                      rd=[br], wr=[b_hid[f]])
            for tb in range(NBC):
                for ch in range(2):
                    p, bp = next_ps()
                    for f in range(32):
                        kb.op("pe", lambda e, f=f, tb=tb, ch=ch, p=p: e.matmul(
                            p[:], hid[:, f, tb * 128:(tb + 1) * 128], wdn[:, f, ch * 512:(ch + 1) * 512],
                            start=(f == 0), stop=(f == 31)),
                            rd=[b_hid[f], b_wdn[f]], wr=[bp])
                    kb.op("dve", lambda e, tb=tb, ch=ch, p=p, ht=ht: e.tensor_tensor(
                        ht[:, tb, ch * 512:(ch + 1) * 512], p[:], ht[:, tb, ch * 512:(ch + 1) * 512], ALU.add),
                        rd=[bp, b_ht], wr=[b_ht])
            for tb in range(NBC):
                kb.op("act", lambda e, tb=tb, ht=ht: e.activation(junk[:], ht[:, tb, :], AF.Square,
                                                                  accum_out=ss[:, 4 + tb:5 + tb]),
                      rd=[b_ht], wr=[b_junk, b_ss])
            kb.op("act", lambda e: e.activation(rstd[:, 4:4 + NBC], ss[:, 4:4 + NBC], AF.Sqrt,
                                                bias=epsc[:, 0:1], scale=1.0 / D),
                  rd=[b_ss, b_eps], wr=[b_rstd])
            kb.op("dve", lambda e: e.reciprocal(rstd[:, 4:4 + NBC], rstd[:, 4:4 + NBC]),
                  rd=[b_rstd], wr=[b_rstd])
            for tb in range(NBC):
                kb.op("dve", lambda e, tb=tb, ht=ht: e.scalar_tensor_tensor(
                    out=ht[:, tb, :], in0=ht[:, tb, :], scalar=rstd[:, 4 + tb:5 + tb], in1=gf[:],
                    op0=ALU.mult, op1=ALU.mult), rd=[b_ht, b_rstd, b_gf], wr=[b_ht])
            kb.dma("sp", out[t * TC:(t + 1) * TC, :].rearrange("(b p) c -> p b c", p=128), ht[:],
                   rd=[b_ht])
        phase_end()

    kb.wait_all("sp", kb.all_toks())
    return nc


def _consts():
    k = np.arange(128)[:, None]
    q = np.arange(128)[None, :]
    cur = (k <= q).astype(np.float32)
    prev = (k >= q).astype(np.float32)
    m = np.concatenate([cur, prev, cur, prev], axis=1)
    sel = np.zeros((128, 64), np.float32)
    sel[64 + np.arange(64), np.arange(64)] = 1.0
    return np.ascontiguousarray(m), sel


_MASKCP, _SELF = _consts()


def _ssm_layouts(inp):
    f = np.float32
    lr = np.asarray(inp["ssm_lambda_re"][0], f); li = np.asarray(inp["ssm_lambda_im"][0], f)
    ldt = np.asarray(inp["ssm_log_dt"][0], f)
    br = np.asarray(inp["ssm_b_re"][0], f); bi = np.asarray(inp["ssm_b_im"][0], f)
    cr = np.asarray(inp["ssm_c_re"][0], f); ci = np.asarray(inp["ssm_c_im"][0], f)
    dd = np.asarray(inp["ssm_d"][0], f)
    part = np.arange(128)
    g8, h = part // 16, part % 16
    gq = np.arange(4)
    gidx = 8 * gq[None, :] + g8[:, None]
    o = {}
    o["lamC_re"] = lr[gidx]
    o["lamC_im"] = li[gidx]
    o["dtC"] = ldt[gidx]
    o["bC_re"] = br[gidx, :, h[:, None]]
    o["bC_im"] = bi[gidx, :, h[:, None]]
    o["dcolC"] = dd[gidx, h[:, None]]
    o["k7"] = np.broadcast_to((7 - np.arange(8, dtype=f))[None, :], (128, 8))
    o["k9"] = np.broadcast_to(np.arange(9, dtype=f)[None, :], (128, 9))
    o["evenodd"] = (g8[:, None] % 4 == np.arange(4)[None, :]).astype(f)
    o["bmask"] = (g8[:, None] == np.arange(8)[None, :]).astype(f)
    half, p = part // 64, part % 64
    o["lamP_re"] = lr[:, p].T
    o["lamP_im"] = li[:, p].T
    o["dtP"] = np.broadcast_to(ldt[None, :], (128, 32))
    crP = np.transpose(cr, (2, 0, 1))[p]
    ciP = np.transpose(ci, (2, 0, 1))[p]
    hm = (half == 0)[:, None, None]
    o["cU"] = np.where(hm, crP, ciP)
    o["cW"] = np.where(hm, ciP, crP)
    o["sgn"] = np.where(half == 0, 1.0, -1.0).astype(f)[:, None]
    ps_ = np.zeros((128, 128), f)
    ps_[part, (part + 64) % 128] = 1.0
    o["pswap"] = ps_
    return {k: np.ascontiguousarray(np.asarray(v, f)) for k, v in o.items()}


def host_inputs(inp, b, L=SEQ):
    f = np.float32
    rep = lambda v: np.ascontiguousarray(np.broadcast_to(np.asarray(v, f).reshape(1, -1), (128, v.size)))
    m = {
        "x": np.ascontiguousarray(np.asarray(inp["x"][b, :L], f)),
        "w_in": np.ascontiguousarray(np.asarray(inp["w_in"][0], f)),
        "g1rep": rep(np.asarray(inp["norm1_g"][0])),
        "g2rep": rep(np.asarray(inp["norm2_g"][0])),
        "gfrep": rep(np.asarray(inp["final_g"])),
        "bgT": np.ascontiguousarray(np.asarray(inp["b_gate"][0], f).reshape(24, 128).T),
        "w_up": np.ascontiguousarray(np.asarray(inp["w_up"][0], f)),
        "w_down": np.ascontiguousarray(np.asarray(inp["w_down"][0], f)),
        "ident": np.eye(128, dtype=f),
        "mem": np.ascontiguousarray(np.asarray(inp["mem"][b], f)),
        "gmrep": rep(np.asarray(inp["mem_norm_g"][0])),
        "w_mem_kv": np.ascontiguousarray(np.asarray(inp["w_mem_kv"][0], f)),
        "w_glu": np.ascontiguousarray(np.asarray(inp["w_glu"][0], f)),
        "bgluT": np.ascontiguousarray(np.asarray(inp["b_glu"][0], f).reshape(4, 128).T),
        "w_ssm_br": np.ascontiguousarray(np.asarray(inp["w_ssm_br"][0], f)),
        "w_attn_br": np.ascontiguousarray(np.asarray(inp["w_attn_br"][0], f)),
        "w_mem_br": np.ascontiguousarray(np.asarray(inp["w_mem_br"][0], f)),
        "w_o": np.ascontiguousarray(np.asarray(inp["w_o"][0], f)),
        "maskcp": _MASKCP,
        **_ssm_layouts(inp),
        "selfm": _SELF,
    }
    return m


def kernel(**inputs):
    nc = build(SEQ)
    in_maps = [host_inputs(inputs, b) for b in range(NB)]
    res = run_bass_kernel_spmd(nc, in_maps, core_ids=list(range(NB)))
    return np.stack([np.asarray(r["out"], np.float32) for r in res.results], axis=0)
```

```python
import numpy as np
import ml_dtypes
import concourse.bass as bass
import concourse.mybir as mybir
from concourse.bass_utils import run_bass_kernel_spmd

F32 = mybir.dt.float32
BF16 = mybir.dt.bfloat16
AF = mybir.ActivationFunctionType
ALU = mybir.AluOpType

D = 1024
SEQ = 8192
NB = 8
INW = 6400
DFF = 4096
MEM = 256
EPS = 1e-6
TT = 512


class Buf:
    __slots__ = ("w", "r")

    def __init__(self):
        self.w = None
        self.r = {}


def bufs(n):
    return [Buf() for _ in range(n)]


class KB:
    def __init__(self, nc):
        self.nc = nc
        self.eng = {"pe": nc.tensor, "act": nc.scalar, "dve": nc.vector,
                    "pool": nc.gpsimd, "sp": nc.sync}
        self.sem = {e: nc.alloc_semaphore("s_" + e) for e in self.eng}
        self.cnt = {e: 0 for e in self.eng}
        self.seen = {e: {} for e in self.eng}
        self.dq = {}
        for q in ("sp", "pool", "act"):
            self.dq[q] = [[("d_%s%d" % (q, i)), nc.alloc_semaphore("d_%s%d" % (q, i)), 0]
                          for i in range(6)]
        self.dqi = {q: 0 for q in self.dq}
        self.ndma = 0

    def _deps(self, rd, wr):
        deps = []
        for b in rd:
            if b.w is not None:
                deps.append(b.w)
        for b in wr:
            if b.w is not None:
                deps.append(b.w)
            deps.extend(b.r.values())
        return deps

    def _wait(self, e, deps):
        seen = self.seen[e]
        eng = self.eng[e]
        for (sname, sem, val) in deps:
            if seen.get(sname, 0) < val:
                eng.wait_ge(sem, val)
                seen[sname] = val

    def op(self, e, fn, rd=(), wr=()):
        deps = self._deps(rd, wr)
        if e == "pe":
            deps = [d_ for d_ in deps if d_[0] != "s_pe"]
        self._wait(e, deps)
        inst = fn(self.eng[e])
        self.cnt[e] += 1
        inst.then_inc(self.sem[e], 1)
        tok = ("s_" + e, self.sem[e], self.cnt[e])
        for b in wr:
            b.w = tok
            b.r = {}
        for b in rd:
            b.r[tok[0]] = tok
        return tok

    def dma(self, q, out, in_, rd=(), wr=(), **kw):
        deps = self._deps(rd, wr)
        slots = self.dq[q]
        i = self.dqi[q]
        self.dqi[q] = (i + 1) % len(slots)
        s = slots[i]
        if s[2] > 0:
            deps.append((s[0], s[1], 16 * s[2]))
        self._wait(q, deps)
        inst = self.eng[q].dma_start(out=out, in_=in_, **kw)
        inst.then_inc(s[1], 16)
        s[2] += 1
        tok = (s[0], s[1], 16 * s[2])
        for b in wr:
            b.w = tok
            b.r = {}
        for b in rd:
            b.r[tok[0]] = tok
        self.ndma += 1
        return tok

    def wait_all(self, e, toks):
        self._wait(e, toks)

    def all_toks(self):
        toks = []
        for q in self.dq:
            for s in self.dq[q]:
                if s[2] > 0:
                    toks.append((s[0], s[1], 16 * s[2]))
        for e in self.eng:
            if self.cnt[e] > 0:
                toks.append(("s_" + e, self.sem[e], self.cnt[e]))
        return toks

    def barrier(self):
        toks = self.all_toks()
        for e in self.eng:
            self._wait(e, toks)


def build(L=SEQ, phases="A1234C", debug=False):
    nc = bass.Bass("TRN2", target_bir_lowering=False)
    kb = KB(nc)
    NT = L // TT

    def din(name, shape, dt=F32):
        return nc.dram_tensor(name, list(shape), dt, kind="ExternalInput").ap()

    x = din("x", [L, D])
    w_in = din("w_in", [D, INW])
    g1rep = din("g1rep", [128, D])
    g2rep = din("g2rep", [128, D])
    gfrep = din("gfrep", [128, D])
    bgT = din("bgT", [128, 24])
    w_up = din("w_up", [D, DFF])
    w_down = din("w_down", [DFF, D])
    ident_d = din("ident", [128, 128])
    mem_d = din("mem", [MEM, D])
    gmrep = din("gmrep", [128, D])
    w_mem_kv = din("w_mem_kv", [D, 1024])
    w_glu = din("w_glu", [512, 512])
    bgluT = din("bgluT", [128, 4])
    w_ssm_br = din("w_ssm_br", [512, D])
    w_attn_br = din("w_attn_br", [256, D])
    w_mem_br = din("w_mem_br", [512, D])
    w_o = din("w_o", [D, D])
    maskcp_d = din("maskcp", [128, 512])
    lamC_re_d = din("lamC_re", [128, 4, 64]); lamC_im_d = din("lamC_im", [128, 4, 64])
    dtC_d = din("dtC", [128, 4])
    bC_re_d = din("bC_re", [128, 4, 64]); bC_im_d = din("bC_im", [128, 4, 64])
    dcolC_d = din("dcolC", [128, 4])
    k7_d = din("k7", [128, 8]); k9_d = din("k9", [128, 9])
    eo_d = din("evenodd", [128, 4])
    bmask_d = din("bmask", [128, 8])
    lamP_re_d = din("lamP_re", [128, 32]); lamP_im_d = din("lamP_im", [128, 32])
    dtP_d = din("dtP", [128, 32])
    cU_d = din("cU", [128, 32, 16]); cW_d = din("cW", [128, 32, 16])
    sgn_d = din("sgn", [128, 1])
    pswap_d = din("pswap", [128, 128])
    self_d = din("selfm", [128, 64])
    out = nc.dram_tensor("out", [L, D], F32, kind="ExternalOutput").ap()
    zT = nc.dram_tensor("zT", [INW, L], BF16, kind="ExternalOutput" if debug else "Internal").ap()
    hscr = nc.dram_tensor("hscr", [L, D], F32, kind="ExternalOutput" if debug else "Internal").ap()
    brT = nc.dram_tensor("brT", [1280, L], BF16, kind="ExternalOutput" if debug else "Internal").ap()

    def sb(name, shape, dt):
        return nc.alloc_sbuf_tensor(name, list(shape), dt).ap()

    identf = sb("identf", [128, 128], F32)
    identb = sb("identb", [128, 128], BF16)
    b_identf, b_identb = Buf(), Buf()
    kb.dma("sp", identf[:], ident_d[:, :], wr=[b_identf])
    kb.op("dve", lambda e: e.tensor_copy(identb[:], identf[:]), rd=[b_identf], wr=[b_identb])

    NPS = 8
    ps = [nc.alloc_psum_tensor("ps%d" % i, [128, 512], F32).ap() for i in range(NPS)]
    b_ps = bufs(NPS)
    psi = [0]

    def next_ps():
        i = psi[0]
        psi[0] = (i + 1) % NPS
        return ps[i], b_ps[i]

    ss = sb("ss", [128, 8], F32)
    rstd = sb("rstd", [128, 8], F32)
    b_ss, b_rstd = Buf(), Buf()
    junk = sb("junk", [128, D], BF16)
    b_junk = Buf()
    epsc = sb("epsc", [128, 1], F32)
    b_eps = Buf()
    kb.op("dve", lambda e: e.memset(epsc[:], EPS), wr=[b_eps])

    def rms_stats(xt, b_xt, nblk):
        for b in range(nblk):
            kb.op("act", lambda e, b=b: e.activation(junk[:], xt[:, b, :], AF.Square,
                                                     accum_out=ss[:, b:b + 1]),
                  rd=[b_xt], wr=[b_junk, b_ss])
        kb.op("act", lambda e: e.activation(rstd[:, 0:nblk], ss[:, 0:nblk], AF.Sqrt,
                                            bias=epsc[:, 0:1], scale=1.0 / D),
              rd=[b_ss, b_eps], wr=[b_rstd])
        kb.op("dve", lambda e: e.reciprocal(rstd[:, 0:nblk], rstd[:, 0:nblk]),
              rd=[b_rstd], wr=[b_rstd])

    def norm_p1(xt, b_xt, grep, b_grep, nb_t, b_nb, nblk=4):
        rms_stats(xt, b_xt, nblk)
        for b in range(nblk):
            kb.op("dve", lambda e, b=b: e.scalar_tensor_tensor(
                out=nb_t[:, b, :], in0=xt[:, b, :], scalar=rstd[:, b:b + 1], in1=grep[:],
                op0=ALU.mult, op1=ALU.mult), rd=[b_xt, b_rstd, b_grep], wr=[b_nb])

    def norm_p2(nb_t, b_nb, nT, b_nT, nblk=4):
        for ct in range(8):
            p, bp = next_ps()
            pb = p.bitcast(BF16)
            for b in range(nblk):
                kb.op("pe", lambda e, b=b, ct=ct, pb=pb: e.transpose(
                    pb[:, b * 128:(b + 1) * 128], nb_t[:, b, ct * 128:(ct + 1) * 128], identb[:]),
                    rd=[b_nb, b_identb], wr=[bp])
            if ct % 2 == 0:
                kb.op("act", lambda e, ct=ct, pb=pb: e.copy(nT[:, ct, :], pb[:, 0:nblk * 128]),
                      rd=[bp], wr=[b_nT[ct]])
            else:
                kb.op("dve", lambda e, ct=ct, pb=pb: e.tensor_copy(nT[:, ct, :], pb[:, 0:nblk * 128]),
                      rd=[bp], wr=[b_nT[ct]])

    def norm_to_T(xt, b_xt, grep, b_grep, nb_t, b_nb, nT, b_nT, nblk=4):
        norm_p1(xt, b_xt, grep, b_grep, nb_t, b_nb, nblk)
        norm_p2(nb_t, b_nb, nT, b_nT, nblk)

    mark = (nc.sbuf_base, nc.sbuf_top)

    def phase_end():
        kb.barrier()
        nc.sbuf_base, nc.sbuf_top = mark

    if "A" in phases:
        win = sb("win", [128, 8, INW], BF16)
        b_win = bufs(8)
        WCH = 640
        b_winc = bufs(INW // WCH)
        for cc in range(INW // WCH):
            kb.dma("pool", win[:, :, cc * WCH:(cc + 1) * WCH],
                   w_in[:, cc * WCH:(cc + 1) * WCH].rearrange("(k p) c -> p k c", p=128), wr=[b_winc[cc]],
                   max_dma_last_dim=2560)
        g1 = sb("g1", [128, D], F32)
        b_g1 = Buf()
        kb.dma("sp", g1[:], g1rep[:, :], wr=[b_g1])
        bg = sb("bg", [128, 24], F32)
        b_bg = Buf()
        kb.dma("sp", bg[:], bgT[:, :], wr=[b_bg])
        xts = [sb("xtA%d" % i, [128, 4, D], F32) for i in range(2)]
        b_xts = bufs(2)
        nbts = [sb("nbtA%d" % i, [128, 4, D], BF16) for i in range(2)]
        b_nbts = bufs(2)
        nTs = [sb("nTA%d" % i, [128, 8, TT], BF16) for i in range(2)]
        b_nTs = [bufs(8) for _ in range(2)]
        NST = 4
        zst = [sb("zst%d" % i, [128, TT], BF16) for i in range(NST)]
        b_zst = bufs(NST)

        def load_x(t):
            kb.dma("sp", xts[t % 2][:], x[t * TT:(t + 1) * TT, :].rearrange("(b p) c -> p b c", p=128),
                   wr=[b_xts[t % 2]])

        def normA(t):
            norm_to_T(xts[t % 2], b_xts[t % 2], g1, b_g1, nbts[t % 2], b_nbts[t % 2], nTs[t % 2], b_nTs[t % 2])

        def normA1(t):
            norm_p1(xts[t % 2], b_xts[t % 2], g1, b_g1, nbts[t % 2], b_nbts[t % 2])

        def normA2(t):
            norm_p2(nbts[t % 2], b_nbts[t % 2], nTs[t % 2], b_nTs[t % 2])

        load_x(0)
        if NT > 1:
            load_x(1)
        normA(0)
        zi = 0
        for t in range(NT):
            nT, b_nT = nTs[t % 2], b_nTs[t % 2]
            for co in range(INW // 128):
                if co == 3 and t + 1 < NT:
                    normA1(t + 1)
                if co == 16 and t + 1 < NT:
                    normA2(t + 1)
                if co == 20 and t + 2 < NT:
                    load_x(t + 2)
                p, bp = next_ps()
                for kt in range(8):
                    kb.op("pe", lambda e, kt=kt, co=co, p=p: e.matmul(
                        p[:], win[:, kt, co * 128:(co + 1) * 128], nT[:, kt, :],
                        start=(kt == 0), stop=(kt == 7)),
                        rd=[b_winc[(co * 128) // WCH], b_nT[kt]], wr=[bp])
                z, bz = zst[zi % NST], b_zst[zi % NST]
                zi += 1
                if co >= 26:
                    g = co - 26
                    kb.op("act", lambda e, z=z, p=p, g=g: e.activation(
                        z[:], p[:], AF.Sigmoid, bias=bg[:, g:g + 1], scale=1.0),
                        rd=[bp, b_bg], wr=[bz])
                else:
                    kb.op("dve", lambda e, z=z, p=p: e.tensor_copy(z[:], p[:]), rd=[bp], wr=[bz])
                kb.dma("sp", zT[co * 128:(co + 1) * 128, t * TT:(t + 1) * TT], z[:], rd=[bz])
        phase_end()

    if "1" in phases:
        wkv = sb("wkv", [128, 8, 1024], BF16)
        b_wkv = bufs(8)
        for kt in range(8):
            kb.dma("pool", wkv[:, kt, :], w_mem_kv[kt * 128:(kt + 1) * 128, :], wr=[b_wkv[kt]],
                   max_dma_last_dim=4096)
        gm = sb("gm", [128, D], F32)
        b_gm = Buf()
        kb.dma("sp", gm[:], gmrep[:, :], wr=[b_gm])
        memt = sb("memt", [128, 2, D], F32)
        b_memt = Buf()
        kb.dma("sp", memt[:], mem_d[:, :].rearrange("(b p) c -> p b c", p=128), wr=[b_memt])
        nbm = sb("nbm", [128, 2, D], BF16)
        b_nbm = Buf()
        nmT = sb("nmT", [128, 8, 256], BF16)
        b_nmT = bufs(8)
        norm_to_T(memt, b_memt, gm, b_gm, nbm, b_nbm, nmT, b_nmT, nblk=2)
        mkT = sb("mkT", [128, 4, 256], BF16)
        b_mkT = Buf()
        mv = sb("mv", [128, 2, 512], BF16)
        b_mv = Buf()
        for h in range(4):
            p, bp = next_ps()
            for kt in range(8):
                kb.op("pe", lambda e, kt=kt, h=h, p=p: e.matmul(
                    p[:, 0:256], wkv[:, kt, h * 128:(h + 1) * 128], nmT[:, kt, :],
                    start=(kt == 0), stop=(kt == 7)), rd=[b_wkv[kt], b_nmT[kt]], wr=[bp])
            kb.op("dve", lambda e, h=h, p=p: e.tensor_copy(mkT[:, h, :], p[:, 0:256]), rd=[bp], wr=[b_mkT])
        for mt in range(2):
            p, bp = next_ps()
            for kt in range(8):
                kb.op("pe", lambda e, kt=kt, mt=mt, p=p: e.matmul(
                    p[:], nmT[:, kt, mt * 128:(mt + 1) * 128], wkv[:, kt, 512:1024],
                    start=(kt == 0), stop=(kt == 7)), rd=[b_wkv[kt], b_nmT[kt]], wr=[bp])
            kb.op("dve", lambda e, mt=mt, p=p: e.tensor_copy(mv[:, mt, :], p[:]), rd=[bp], wr=[b_mv])
        onesb = sb("onesb", [128, 128], BF16)
        b_ones = Buf()
        kb.op("dve", lambda e: e.memset(onesb[:], 1.0), wr=[b_ones])
        mqs = [sb("mq%d" % i, [128, 4, TT], BF16) for i in range(2)]
        b_mqs = bufs(2)
        rdn = [sb("rdn%d" % i, [128, TT], F32) for i in range(2)]
        b_rdn = bufs(2)
        mo = [sb("mo%d" % i, [128, TT], BF16) for i in range(2)]
        b_mo = bufs(2)
        MQ0 = 512 + 3 * 768

        def load_mq(t):
            kb.dma("sp", mqs[t % 2][:], zT[MQ0:MQ0 + 512, t * TT:(t + 1) * TT].rearrange("(h p) t -> p h t", p=128),
                   wr=[b_mqs[t % 2]])

        load_mq(0)
        NPB = 3
        pT = [sb("pTm%d" % i, [128, 2, TT], BF16) for i in range(NPB)]
        b_pT = bufs(NPB)

        def b1_front(t, h, it):
            mq, b_mq = mqs[t % 2], b_mqs[t % 2]
            pt, b_pt = pT[it % NPB], b_pT[it % NPB]
            for mt in range(2):
                p, bp = next_ps()
                kb.op("pe", lambda e, mt=mt, p=p: e.matmul(
                    p[:], mkT[:, h, mt * 128:(mt + 1) * 128], mq[:, h, :], start=True, stop=True),
                    rd=[b_mkT, b_mq], wr=[bp])
                kb.op("act", lambda e, mt=mt, p=p: e.activation(
                    pt[:, mt, :], p[:], AF.Exp, scale=float(128 ** -0.5)), rd=[bp], wr=[b_pt])

        def b1_back(t, h, it):
            pt, b_pt = pT[it % NPB], b_pT[it % NPB]
            po, bpo = next_ps()
            pd, bpd = next_ps()
            for mt in range(2):
                kb.op("pe", lambda e, mt=mt: e.matmul(
                    po[:], mv[:, mt, h * 128:(h + 1) * 128], pt[:, mt, :], start=(mt == 0), stop=(mt == 1)),
                    rd=[b_mv, b_pt], wr=[bpo])
            for mt in range(2):
                kb.op("pe", lambda e, mt=mt: e.matmul(
                    pd[:], onesb[:], pt[:, mt, :], start=(mt == 0), stop=(mt == 1)),
                    rd=[b_ones, b_pt], wr=[bpd])
            rd_, b_rd = rdn[it % 2], b_rdn[it % 2]
            o_, b_o = mo[it % 2], b_mo[it % 2]
            kb.op("dve", lambda e: e.reciprocal(rd_[:], pd[:]), rd=[bpd], wr=[b_rd])
            kb.op("dve", lambda e: e.tensor_tensor(o_[:], po[:], rd_[:], ALU.mult),
                  rd=[bpo, b_rd], wr=[b_o])
            kb.dma("sp", brT[768 + h * 128:768 + (h + 1) * 128, t * TT:(t + 1) * TT], o_[:], rd=[b_o])

        work = [(t, h) for t in range(NT) for h in range(4)]
        prev = None
        for it, (t, h) in enumerate(work):
            if h == 0 and t + 1 < NT:
                load_mq(t + 1)
            b1_front(t, h, it)
            if prev is not None:
                b1_back(*prev)
            prev = (t, h, it)
        b1_back(*prev)
        phase_end()

    if "2" in phases:
        SBK = 2048
        NSB = L // SBK
        maskf = sb("maskf", [128, 512], F32)
        maskb = sb("maskb", [128, 512], BF16)
        b_maskf, b_mask = Buf(), Buf()
        kb.dma("sp", maskf[:], maskcp_d[:, :], wr=[b_maskf])
        kb.op("dve", lambda e: e.tensor_copy(maskb[:], maskf[:]), rd=[b_maskf], wr=[b_mask])
        selff = sb("selff", [128, 64], F32)
        b_self = Buf()
        kb.dma("sp", selff[:], self_d[:, :], wr=[b_self])
        NSET = 3
        QT = [sb("QT%d" % i, [64, SBK], BF16) for i in range(NSET)]
        KT = [sb("KT%d" % i, [64, 2 * SBK], BF16) for i in range(NSET)]
        VT = [sb("VT%d" % i, [64, 2 * SBK], BF16) for i in range(NSET)]
        b_QT, b_KT, b_VT = bufs(NSET), bufs(NSET), bufs(NSET)
        Vaug = [sb("Vaug%d" % i, [128, 32, 128], BF16) for i in range(2)]
        b_Vaug = bufs(2)
        for i in range(2):
            kb.op("pool", lambda e, i=i: e.memset(Vaug[i][:], 1.0), wr=[b_Vaug[i]])
        NPT = 5
        pi_ = [0]
        PTs = [sb("PTa%d" % i, [128, 512], BF16) for i in range(NPT)]
        PTm = [sb("PTb%d" % i, [128, 512], BF16) for i in range(NPT)]
        b_PTs, b_PTm = bufs(NPT), bufs(NPT)
        accs = [sb("acc%d" % i, [128, SBK], F32) for i in range(2)]
        b_accs = bufs(2)
        oTs = [sb("oTs%d" % i, [64, SBK], BF16) for i in range(2)]
        b_oTs = bufs(2)
        DIL = (1, 4, 16)

        class Head:
            pass

        heads = []
        for sbi in range(NSB):
            for j in range(4):
                for g in range(3):
                    h_ = Head()
                    h_.sbi, h_.j, h_.g, h_.d = sbi, j, g, DIL[g]
                    h_.hh = 4 * g + j
                    h_.t0 = sbi * SBK
                    h_.lbb = 1 if sbi > 0 else 0
                    h_.LB = 128 * h_.d * h_.lbb
                    h_.nper = 16 // h_.d
                    h_.idx = len(heads)
                    h_.slot = sbi * 4 + j
                    heads.append(h_)

        def kidx(h_, r, n):
            return r * (h_.nper + h_.lbb) + (n + h_.lbb)

        def kcol(h_, r, n):
            return h_.d * 128 * (n + h_.lbb) + r

        def loads(h_):
            i = h_.idx % NSET
            hh, t0, LB = h_.hh, h_.t0, h_.LB
            kb.dma("sp", QT[i][:, :], zT[512 + hh * 64:512 + (hh + 1) * 64, t0:t0 + SBK], wr=[b_QT[i]])
            kb.dma("sp", KT[i][:, 0:LB + SBK], zT[1280 + hh * 64:1280 + (hh + 1) * 64, t0 - LB:t0 + SBK], wr=[b_KT[i]])
            kb.dma("sp", VT[i][:, 0:LB + SBK], zT[2048 + hh * 64:2048 + (hh + 1) * 64, t0 - LB:t0 + SBK], wr=[b_VT[i]])

        def vaug_build(h_):
            i = h_.idx % NSET
            vt, bv = VT[i], b_VT[i]
            va, bva = Vaug[h_.idx % 2], b_Vaug[h_.idx % 2]
            d = h_.d
            blks = [(r, n) for r in range(d) for n in range(-h_.lbb, h_.nper)]
            for c0 in range(0, len(blks), 8):
                grp = blks[c0:c0 + 8]
                p, bp = next_ps()
                pb = p.bitcast(BF16)
                for ii, (r, n) in enumerate(grp):
                    cb = kcol(h_, r, n)
                    kb.op("pe", lambda e, ii=ii, cb=cb: e.transpose(
                        pb[:, ii * 64:(ii + 1) * 64], vt[:, cb:cb + 127 * d + 1:d], identb[0:64, 0:64]),
                        rd=[bv, b_identb], wr=[bp])
                i0_ = kidx(h_, *grp[0])
                ng = len(grp)
                kb.op("dve", lambda e: e.tensor_copy(
                    va[:, i0_:i0_ + ng, 0:64], pb[:, 0:ng * 64].rearrange("p (n e) -> p n e", e=64)),
                    rd=[bp], wr=[bva])

        def front(h_, o0, half):
            i = h_.idx % NSET
            qt, kt_, bq, bk = QT[i], KT[i], b_QT[i], b_KT[i]
            d = h_.d
            qbs = [(r, n) for r in range(d) for n in range(h_.nper)]
            two = qbs[o0 + 2 * half:o0 + 2 * half + 2]
            pairs = []
            for qi_, (r, n) in enumerate(two):
                pairs.append((2 * half + qi_, r, n, n, 0))
                if n - 1 >= -h_.lbb:
                    pairs.append((2 * half + qi_, r, n, n - 1, 1))
            p, bp = next_ps()
            for si, (qs, r, nq, nk, mt_) in enumerate(pairs):
                qc = d * 128 * nq + r
                kc = kcol(h_, r, nk)
                kb.op("pe", lambda e, si=si, qc=qc, kc=kc: e.matmul(
                    p[:, si * 128:(si + 1) * 128], kt_[:, kc:kc + 127 * d + 1:d], qt[:, qc:qc + 127 * d + 1:d],
                    start=True, stop=True), rd=[bk, bq], wr=[bp])
            npair = len(pairs)
            k_ = pi_[0]
            pi_[0] += 1
            pa, bpa = PTs[k_ % NPT], b_PTs[k_ % NPT]
            pm, bpm = PTm[k_ % NPT], b_PTm[k_ % NPT]
            kb.op("act", lambda e: e.activation(
                pa[:, 0:128 * npair], p[:, 0:128 * npair], AF.Exp, scale=0.125),
                rd=[bp], wr=[bpa])
            meng = "pool" if k_ % 3 == 0 else "dve"
            if [x_[4] for x_ in pairs] == [0, 1, 0, 1]:
                kb.op(meng, lambda e: e.tensor_tensor(
                    pm[:], pa[:], maskb[:], ALU.mult), rd=[bpa, b_mask], wr=[bpm])
            else:
                for si, (qs, r, nq, nk, mt_) in enumerate(pairs):
                    kb.op(meng, lambda e, si=si, mt_=mt_: e.tensor_tensor(
                        pm[:, si * 128:(si + 1) * 128], pa[:, si * 128:(si + 1) * 128],
                        maskb[:, mt_ * 128:(mt_ + 1) * 128], ALU.mult),
                        rd=[bpa, b_mask], wr=[bpm])
            return pairs, pm, bpm

        obank = {}

        def back(h_, o0, half, pairs, pm, bpm):
            va, bva = Vaug[h_.idx % 2], b_Vaug[h_.idx % 2]
            acc, b_acc = accs[h_.slot % 2], b_accs[h_.slot % 2]
            d = h_.d
            if half == 0:
                obank[(h_.idx, o0)] = next_ps()
            po, bpo = obank[(h_.idx, o0)]
            byq = {}
            for si, pr in enumerate(pairs):
                byq.setdefault(pr[0], []).append((si, pr))
            for qs, lst in byq.items():
                for li, (si, (qs_, r, nq, nk, mt_)) in enumerate(lst):
                    vi = kidx(h_, r, nk)
                    kb.op("pe", lambda e, si=si, vi=vi, qs=qs, li=li, nl=len(lst): e.matmul(
                        po[:, qs * 128:(qs + 1) * 128], va[:, vi, :], pm[:, si * 128:(si + 1) * 128],
                        start=(li == 0), stop=(li == nl - 1)), rd=[bva, bpm], wr=[bpo])
            if half == 1:
                if d == 1:
                    av = acc[:, o0 * 128:(o0 + 4) * 128]
                    pv = po[:, :]
                elif d == 4:
                    r = o0 // 4
                    av = acc[:, r:SBK:4]
                    pv = po[:, :]
                else:
                    av = acc[:, :].rearrange("p (i r) -> p r i", r=16)[:, o0:o0 + 4, :]
                    pv = po[:, :].rearrange("p (r i) -> p r i", i=128)
                if h_.g == 0:
                    kb.op("dve", lambda e: e.tensor_copy(av, pv), rd=[bpo], wr=[b_acc])
                else:
                    kb.op("dve", lambda e: e.tensor_tensor(av, pv, av, ALU.add),
                          rd=[bpo, b_acc], wr=[b_acc])

        def norm_closures(h_):
            acc, b_acc = accs[h_.slot % 2], b_accs[h_.slot % 2]
            ot_, b_ot = oTs[h_.slot % 2], b_oTs[h_.slot % 2]
            j, t0 = h_.j, h_.t0
            b_rc = bufs(SBK // 512)
            cl = []

            def f_rec(c):
                cs = slice(c * 512, (c + 1) * 512)
                kb.op("dve", lambda e: e.reciprocal(acc[64:128, cs], acc[64:128, cs]), rd=[b_acc], wr=[b_rc[c]])

            def f_fin(c):
                cs = slice(c * 512, (c + 1) * 512)
                p, bp = next_ps()
                kb.op("pe", lambda e: e.matmul(p[0:64, :], selff[:, :], acc[:, cs], start=True, stop=True),
                      rd=[b_self, b_acc, b_rc[c]], wr=[bp])
                kb.op("dve", lambda e: e.tensor_tensor(ot_[:, cs], p[0:64, :], acc[0:64, cs], ALU.mult),
                      rd=[bp, b_acc], wr=[b_ot])
                if c == SBK // 512 - 1:
                    kb.dma("sp", brT[512 + j * 64:512 + (j + 1) * 64, t0:t0 + SBK], ot_[:, :], rd=[b_ot])

            nchunk = SBK // 512
            for c in range(nchunk + 1):
                def f(c=c):
                    if c < nchunk:
                        f_rec(c)
                    if c >= 1:
                        f_fin(c - 1)
                cl.append(f)
            return cl

        stage_list = [(o0, half) for o0 in range(0, 16, 4) for half in range(2)]
        loads(heads[0])
        if len(heads) > 1:
            loads(heads[1])
        vaug_build(heads[0])
        deferred = []
        for hi_, h_ in enumerate(heads):
            if hi_ + 2 < len(heads):
                loads(heads[hi_ + 2])
            pendq = []
            for si_, (o0, half) in enumerate(stage_list):
                pendq.append((h_, o0, half) + front(h_, o0, half))
                if len(pendq) > 2:
                    back(*pendq.pop(0))
                if si_ == 3 and hi_ + 1 < len(heads):
                    vaug_build(heads[hi_ + 1])
                if deferred and si_ >= 1:
                    deferred.pop(0)()
            while pendq:
                back(*pendq.pop(0))
            while deferred:
                deferred.pop(0)()
            if h_.g == 2:
                deferred = norm_closures(h_)
        while deferred:
            deferred.pop(0)()
        phase_end()

    if "3" in phases:
        TWO_PI = float(2.0 * np.pi)
        MAGIC = 12582912.0
        ctr = [0]

        def tl(shape, dt=F32):
            ctr[0] += 1
            return sb("s3_%d" % ctr[0], shape, dt), Buf()

        def ld(src, shape):
            t_, b_ = tl(shape)
            kb.dma("sp", t_[:], src, wr=[b_])
            return t_, b_

        def flat(a):
            n = len(a.shape)
            if n == 2:
                return a
            if n == 3:
                return a.rearrange("p a b -> p (a b)")
            if n == 4:
                return a.rearrange("p a b c -> p (a b c)")

        def tt(o, bo, a, ba, b, bb_, op, eng="dve"):
            kb.op(eng, lambda e: e.tensor_tensor(o, a, b, op), rd=[ba, bb_], wr=[bo])

        def sincos(ang, b_ang, shape):
            n_, bn = tl(shape)
            sn, bsn = tl(shape)
            cs, bcs = tl(shape)
            r_, br = ang, b_ang
            kb.op("dve", lambda e: e.tensor_scalar(r_, r_, 1.0 / TWO_PI, None, op0=ALU.mult), rd=[br], wr=[br])
            kb.op("dve", lambda e: e.tensor_scalar(n_[:], r_, MAGIC, None, op0=ALU.add), rd=[br], wr=[bn])
            kb.op("dve", lambda e: e.tensor_scalar(n_[:], n_[:], MAGIC, None, op0=ALU.subtract), rd=[bn], wr=[bn])
            tt(n_[:], bn, r_, br, n_[:], bn, ALU.subtract)
            kb.op("act", lambda e: e.activation(sn[:], n_[:], AF.Sin, scale=TWO_PI), rd=[bn], wr=[bsn])
            kb.op("dve", lambda e: e.tensor_scalar(r_, r_, 0.25, None, op0=ALU.add), rd=[br], wr=[br])
            kb.op("dve", lambda e: e.tensor_scalar(n_[:], r_, MAGIC, None, op0=ALU.add), rd=[br], wr=[bn])
            kb.op("dve", lambda e: e.tensor_scalar(n_[:], n_[:], MAGIC, None, op0=ALU.subtract), rd=[bn], wr=[bn])
            tt(n_[:], bn, r_, br, n_[:], bn, ALU.subtract)
            kb.op("act", lambda e: e.activation(cs[:], n_[:], AF.Sin, scale=TWO_PI), rd=[bn], wr=[bcs])
            return cs, bcs, sn, bsn, n_, bn

        pm3 = (nc.sbuf_base, nc.sbuf_top)
        WvM = [sb("WvM%d" % m_, [128, 4, 8, 128], BF16) for m_ in range(4)]
        b_WvM = bufs(4)
        Kc = sb("Kc", [128, 4, 8, 128], BF16); b_Kc = Buf()
        WcT = sb("WcT", [128, 32, 8, 64], BF16); b_WcT = Buf()
        NLV = 7
        TbbP = sb("TbbP", [128, 4, 128], F32); b_TbbP = Buf()
        wglu = sb("wglu", [128, 4, 512], BF16); b_wglu = Buf()
        kb.dma("pool", wglu[:], w_glu.rearrange("(k p) c -> p k c", p=128), wr=[b_wglu], max_dma_last_dim=4096)
        bglu = sb("bglu", [128, 4], F32); b_bglu = Buf()
        kb.dma("sp", bglu[:], bgluT[:, :], wr=[b_bglu])
        pm3b = (nc.sbuf_base, nc.sbuf_top)

        lamr, b_lamr = ld(lamC_re_d[:, :, :], [128, 4, 64])
        lami, b_lami = ld(lamC_im_d[:, :, :], [128, 4, 64])
        dtc, b_dtc = ld(dtC_d[:, :], [128, 4])
        bre, b_bre = ld(bC_re_d[:, :, :], [128, 4, 64])
        bim, b_bim = ld(bC_im_d[:, :, :], [128, 4, 64])
        k7, b_k7 = ld(k7_d[:, :], [128, 8])
        eo, b_eo = ld(eo_d[:, :], [128, 4])

        kb.op("act", lambda e: e.activation(dtc[:], dtc[:], AF.Exp), rd=[b_dtc], wr=[b_dtc])
        LRD, b_LRD = tl([128, 4, 64]); LID, b_LID = tl([128, 4, 64])
        dtb = dtc[:, :].unsqueeze(2).to_broadcast([128, 4, 64])
        tt(LRD[:], b_LRD, lamr[:], b_lamr, dtb, b_dtc, ALU.mult)
        tt(LID[:], b_LID, lami[:], b_lami, dtb, b_dtc, ALU.mult)
        mag1, b_mag1 = tl([128, 4, 64])
        kb.op("act", lambda e: e.activation(mag1[:], LRD[:], AF.Exp), rd=[b_LRD], wr=[b_mag1])
        LID2, b_LID2 = tl([128, 4, 64])
        kb.op("dve", lambda e: e.tensor_copy(LID2[:], LID[:]), rd=[b_LID], wr=[b_LID2])
        c1, b_c1, s1, b_s1, _, _ = sincos(LID2[:], b_LID2, [128, 4, 64])
        are, b_are = tl([128, 4, 64]); aim, b_aim = tl([128, 4, 64])
        tt(are[:], b_are, mag1[:], b_mag1, c1[:], b_c1, ALU.mult)
        tt(aim[:], b_aim, mag1[:], b_mag1, s1[:], b_s1, ALU.mult)
        kb.op("dve", lambda e: e.tensor_scalar(are[:], are[:], -1.0, None, op0=ALU.add), rd=[b_are], wr=[b_are])
        den, b_den = tl([128, 4, 64]); t1, b_t1 = tl([128, 4, 64]); t2, b_t2 = tl([128, 4, 64])
        tt(den[:], b_den, lamr[:], b_lamr, lamr[:], b_lamr, ALU.mult)
        tt(t1[:], b_t1, lami[:], b_lami, lami[:], b_lami, ALU.mult)
        tt(den[:], b_den, den[:], b_den, t1[:], b_t1, ALU.add)
        kb.op("dve", lambda e: e.reciprocal(den[:], den[:]), rd=[b_den], wr=[b_den])
        cre, b_cre = tl([128, 4, 64]); cim, b_cim = tl([128, 4, 64])
        tt(t1[:], b_t1, are[:], b_are, lamr[:], b_lamr, ALU.mult)
        tt(t2[:], b_t2, aim[:], b_aim, lami[:], b_lami, ALU.mult)
        tt(t1[:], b_t1, t1[:], b_t1, t2[:], b_t2, ALU.add)
        tt(cre[:], b_cre, t1[:], b_t1, den[:], b_den, ALU.mult)
        tt(t1[:], b_t1, aim[:], b_aim, lamr[:], b_lamr, ALU.mult)
        tt(t2[:], b_t2, are[:], b_are, lami[:], b_lami, ALU.mult)
        tt(t1[:], b_t1, t1[:], b_t1, t2[:], b_t2, ALU.subtract)
        tt(cim[:], b_cim, t1[:], b_t1, den[:], b_den, ALU.mult)
        bbcat, b_bbcat = tl([128, 4, 128])
        tt(t1[:], b_t1, cre[:], b_cre, bre[:], b_bre, ALU.mult)
        tt(t2[:], b_t2, cim[:], b_cim, bim[:], b_bim, ALU.mult)
        tt(bbcat[:, :, 0:64], b_bbcat, t1[:], b_t1, t2[:], b_t2, ALU.subtract)
        tt(t1[:], b_t1, cre[:], b_cre, bim[:], b_bim, ALU.mult)
        tt(t2[:], b_t2, cim[:], b_cim, bre[:], b_bre, ALU.mult)
        tt(bbcat[:, :, 64:128], b_bbcat, t1[:], b_t1, t2[:], b_t2, ALU.add)
        SH = [128, 4, 8, 64]
        argR, b_argR = tl(SH); argI, b_argI = tl(SH)
        k7b = k7[:, :].unsqueeze(1).unsqueeze(3).to_broadcast(SH)
        tt(argR[:], b_argR, LRD[:, :, :].unsqueeze(2).to_broadcast(SH), b_LRD, k7b, b_k7, ALU.mult)
        tt(argI[:], b_argI, LID[:, :, :].unsqueeze(2).to_broadcast(SH), b_LID, k7b, b_k7, ALU.mult)
        kb.op("act", lambda e: e.activation(flat(argR[:]), flat(argR[:]), AF.Exp), rd=[b_argR], wr=[b_argR])
        ck, b_ck, sk, b_sk, nsc, b_nsc = sincos(flat(argI[:]), b_argI, [128, 2048])
        pre, b_pre = ck, b_ck
        pim, b_pim = sk, b_sk
        tt(pre[:], b_pre, flat(argR[:]), b_argR, ck[:], b_ck, ALU.mult)
        tt(pim[:], b_pim, flat(argR[:]), b_argR, sk[:], b_sk, ALU.mult)
        pre4 = pre[:, :].rearrange("p (a b c) -> p a b c", a=4, b=8)
        pim4 = pim[:, :].rearrange("p (a b c) -> p a b c", a=4, b=8)
        bbr_b = bbcat[:, :, 0:64].unsqueeze(2).to_broadcast(SH)
        bbi_b = bbcat[:, :, 64:128].unsqueeze(2).to_broadcast(SH)
        wvr, b_wvr = argR, b_argR
        wvi, b_wvi = argI, b_argI
        t4, b_t4 = nsc[:, :].rearrange("p (a b c) -> p a b c", a=4, b=8), b_nsc
        tt(wvr[:], b_wvr, pre4, b_pre, bbr_b, b_bbcat, ALU.mult)
        tt(t4[:], b_t4, pim4, b_pim, bbi_b, b_bbcat, ALU.mult)
        tt(wvr[:], b_wvr, wvr[:], b_wvr, t4[:], b_t4, ALU.subtract)
        tt(wvi[:], b_wvi, pre4, b_pre, bbi_b, b_bbcat, ALU.mult)
        tt(t4[:], b_t4, pim4, b_pim, bbr_b, b_bbcat, ALU.mult)
        tt(wvi[:], b_wvi, wvi[:], b_wvi, t4[:], b_t4, ALU.add)
        for (W_, bW, col) in [(WvM[m_], b_WvM[m_], m_) for m_ in range(4)]:
            kb.op("dve", lambda e, W_=W_, col=col: e.tensor_scalar(W_[:, :, :, 0:64], wvr[:], eo[:, col:col + 1], None, op0=ALU.mult),
                  rd=[b_wvr, b_eo], wr=[bW])
            kb.op("dve", lambda e, W_=W_, col=col: e.tensor_scalar(W_[:, :, :, 64:128], wvi[:], eo[:, col:col + 1], None, op0=ALU.mult),
                  rd=[b_wvi, b_eo], wr=[bW])
        for gq in range(4):
            p, bp = next_ps()
            kb.op("pe", lambda e, gq=gq, p=p: e.transpose(p[:, 0:128], bbcat[:, gq, :], identf[:]),
                  rd=[b_bbcat, b_identf], wr=[bp])
            kb.op("dve", lambda e, gq=gq, p=p: e.tensor_copy(TbbP[:, gq, :], p[:, 0:128]), rd=[bp], wr=[b_TbbP])

        kb.barrier()
        nc.sbuf_base, nc.sbuf_top = pm3b
        Rot = sb("Rot", [128, NLV, 32, 128], BF16); b_Rot = Buf()
        pm3b = (nc.sbuf_base, nc.sbuf_top)
        k9, b_k9 = ld(k9_d[:, :], [128, 9])
        bmask, b_bmask = ld(bmask_d[:, :], [128, 8])
        pswap, b_pswap = ld(pswap_d[:, :], [128, 128])
        sgn, b_sgn = ld(sgn_d[:, :], [128, 1])
        dcol, b_dcol = ld(dcolC_d[:, :], [128, 4])
        lpr, b_lpr = ld(lamP_re_d[:, :], [128, 32]); lpi, b_lpi = ld(lamP_im_d[:, :], [128, 32])
        dtp, b_dtp = ld(dtP_d[:, :], [128, 32])
        cU, b_cU = ld(cU_d[:, :, :], [128, 32, 16]); cW, b_cW = ld(cW_d[:, :, :], [128, 32, 16])
        kb.op("act", lambda e: e.activation(dtp[:], dtp[:], AF.Exp), rd=[b_dtp], wr=[b_dtp])
        tt(lpr[:], b_lpr, lpr[:], b_lpr, dtp[:], b_dtp, ALU.mult)
        tt(lpi[:], b_lpi, lpi[:], b_lpi, dtp[:], b_dtp, ALU.mult)
        S9 = [128, 32, 9]
        a9r, b_a9r = tl(S9); a9i, b_a9i = tl(S9)
        k9b = k9[:, :].unsqueeze(1).to_broadcast(S9)
        tt(a9r[:], b_a9r, lpr[:, :].unsqueeze(2).to_broadcast(S9), b_lpr, k9b, b_k9, ALU.mult)
        tt(a9i[:], b_a9i, lpi[:, :].unsqueeze(2).to_broadcast(S9), b_lpi, k9b, b_k9, ALU.mult)
        kb.op("act", lambda e: e.activation(flat(a9r[:]), flat(a9r[:]), AF.Exp), rd=[b_a9r], wr=[b_a9r])
        c9, b_c9, s9, b_s9, _, _ = sincos(flat(a9i[:]), b_a9i, [128, 288])
        P9r, b_P9r = tl(S9); P9i, b_P9i = tl(S9)
        tt(flat(P9r[:]), b_P9r, flat(a9r[:]), b_a9r, c9[:], b_c9, ALU.mult)
        tt(flat(P9i[:]), b_P9i, flat(a9r[:]), b_a9r, s9[:], b_s9, ALU.mult)
        SE = [128, 32, 9, 16]
        E, b_E = tl(SE); E2, b_E2 = tl(SE)
        tt(E[:], b_E, cU[:, :, :].unsqueeze(2).to_broadcast(SE), b_cU, P9r[:, :, :].unsqueeze(3).to_broadcast(SE), b_P9r, ALU.mult)
        tt(E2[:], b_E2, cW[:, :, :].unsqueeze(2).to_broadcast(SE), b_cW, P9i[:, :, :].unsqueeze(3).to_broadcast(SE), b_P9i, ALU.mult)
        kb.op("dve", lambda e: e.scalar_tensor_tensor(out=flat(E[:]), in0=flat(E[:]), scalar=sgn[:, 0:1], in1=flat(E2[:]),
                                                      op0=ALU.mult, op1=ALU.subtract), rd=[b_E, b_E2, b_sgn], wr=[b_E])
        kb.op("pool", lambda e: e.memset(WcT[:], 0.0), wr=[b_WcT])
        for par in range(4):
            kb.op("dve", lambda e, par=par: e.tensor_copy(WcT[:, par:32:4, :, par * 16:(par + 1) * 16], E[:, par:32:4, 1:9, :]),
                  rd=[b_E], wr=[b_WcT])
        Ar, b_Ar = tl([128, NLV, 32]); Ai, b_Ai = tl([128, NLV, 32]); tq, b_tq = tl([128, 32])
        kb.op("dve", lambda e: e.tensor_copy(Ar[:, 0, :], P9r[:, :, 8]), rd=[b_P9r], wr=[b_Ar])
        kb.op("dve", lambda e: e.tensor_copy(Ai[:, 0, :], P9i[:, :, 8]), rd=[b_P9i], wr=[b_Ai])
        for l in range(1, NLV):
            tt(tq[:], b_tq, Ai[:, l - 1, :], b_Ai, Ai[:, l - 1, :], b_Ai, ALU.mult)
            tt(Ar[:, l, :], b_Ar, Ar[:, l - 1, :], b_Ar, Ar[:, l - 1, :], b_Ar, ALU.mult)
            tt(Ar[:, l, :], b_Ar, Ar[:, l, :], b_Ar, tq[:], b_tq, ALU.subtract)
            tt(tq[:], b_tq, Ar[:, l - 1, :], b_Ar, Ai[:, l - 1, :], b_Ai, ALU.mult)
            kb.op("dve", lambda e, l=l: e.tensor_scalar(Ai[:, l, :], tq[:], 2.0, None, op0=ALU.mult), rd=[b_tq], wr=[b_Ai])
        Bi, b_Bi = tl([128, NLV, 32])
        kb.op("dve", lambda e: e.tensor_scalar(Bi[:], Ai[:], sgn[:, 0:1], None, op0=ALU.mult), rd=[b_Ai, b_sgn], wr=[b_Bi])
        rts = [tl([128, 128]) for _ in range(4)]
        b_RotL = bufs(NLV)
        ri = 0
        for l in range(NLV):
            for g in range(32):
                rt, b_rt = rts[ri % 4]
                ri += 1
                kb.op("act", lambda e, l=l, g=g, rt=rt: e.activation(rt[:], identf[:], AF.Copy, scale=Ar[:, l, g:g + 1]),
                      rd=[b_identf, b_Ar], wr=[b_rt])
                kb.op("dve", lambda e, l=l, g=g, rt=rt: e.scalar_tensor_tensor(out=Rot[:, l, g, :], in0=pswap[:], scalar=Bi[:, l, g:g + 1],
                                                                                 in1=rt[:], op0=ALU.mult, op1=ALU.add),
                      rd=[b_pswap, b_Bi, b_rt], wr=[b_RotL[l]])
        b_Rot = Buf()
        kb.op("dve", lambda e: e.tensor_copy(Rot[:, 0, 0, 0:1], Rot[:, 0, 0, 0:1]), rd=b_RotL, wr=[b_Rot])
        KtT, b_KtT = tl([128, 128]); KtS, b_KtS = tl([128, 8, 16]); Kc32, b_Kc32 = tl([128, 8, 8, 16])
        for gq in range(4):
            p, bp = next_ps()
            for g8 in range(8):
                g = gq * 8 + g8
                kb.op("pe", lambda e, g=g, g8=g8, gq=gq, p=p: e.matmul(
                    p[:, g8 * 16:(g8 + 1) * 16], E[:, g, 0:8, :].rearrange("p a b -> p (a b)"), TbbP[:, gq, g8 * 16:(g8 + 1) * 16],
                    start=True, stop=True), rd=[b_E, b_TbbP], wr=[bp])
            kb.op("dve", lambda e, p=p: e.tensor_copy(KtT[:], p[:, 0:128]), rd=[bp], wr=[b_KtT])
            p2, bp2 = next_ps()
            kb.op("pe", lambda e, p2=p2: e.transpose(p2[:, 0:128], KtT[:], identf[:]), rd=[b_KtT, b_identf], wr=[bp2])
            kb.op("dve", lambda e, p2=p2: e.tensor_copy(KtS[:].rearrange("p a b -> p (a b)"), p2[:, 0:128]), rd=[bp2], wr=[b_KtS])
            SK = [128, 8, 8, 16]
            tt(Kc32[:], b_Kc32, KtS[:, :, :].unsqueeze(2).to_broadcast(SK), b_KtS,
               bmask[:, :].unsqueeze(1).unsqueeze(3).to_broadcast(SK), b_bmask, ALU.mult)
            kb.op("dve", lambda e, gq=gq: e.scalar_tensor_tensor(
                out=Kc32[:, 0, :, :].rearrange("p a b -> p (a b)"), in0=identf[:], scalar=dcol[:, gq:gq + 1],
                in1=Kc32[:, 0, :, :].rearrange("p a b -> p (a b)"), op0=ALU.mult, op1=ALU.add),
                rd=[b_identf, b_dcol, b_Kc32], wr=[b_Kc32])
            kb.op("dve", lambda e, gq=gq: e.tensor_copy(Kc[:, gq, :, :].rearrange("p a b -> p (a b)"), flat(Kc32[:])),
                  rd=[b_Kc32], wr=[b_Kc])
        kb.barrier()
        nc.sbuf_base, nc.sbuf_top = pm3b

        CH = 1024
        J = CH // 8
        NCH = L // CH
        uTs = [sb("uT%d" % i, [128, 4, CH], BF16) for i in range(2)]
        b_uTs = bufs(2)
        uTd = sb("uTd", [128, 4, 8, J], BF16); b_uTd = bufs(4)
        Xf = sb("Xf", [128, 32, J], F32); b_Xf = bufs(8)
        Xb = sb("Xb", [128, 32, J], BF16); b_Xb = bufs(8)
        Vl = sb("Vl", [128, 32], F32); b_Vl = Buf()
        Sin = sb("Sin", [128, 32], F32); b_Sin = Buf()
        kb.op("dve", lambda e: e.memset(Sin[:], 0.0), wr=[b_Sin])
        ygTs = [sb("ygT%d" % i, [128, 4, CH], BF16) for i in range(2)]
        b_ygTs = [bufs(4) for _ in range(2)]
        gt1, b_gt1 = sb("gt1", [128, 512], F32), Buf()
        gt2, b_gt2 = gt1, b_gt1
        gsg, b_gsg = gt1, b_gt1
        y2s = [sb("y2s%d" % i, [128, 512], BF16) for i in range(2)]
        b_y2s = bufs(2)
        sg2, b_sg2 = sb("sg2", [128, 512], BF16), Buf()

        def load_u(c):
            kb.dma("sp", uTs[c % 2][:], zT[0:512, c * CH:(c + 1) * CH].rearrange("(k p) t -> p k t", p=128), wr=[b_uTs[c % 2]])

        yi_ = [0]
        pending_glu = []

        def emit_glu(c):
            ygT, b_ygT = ygTs[c % 2], b_ygTs[c % 2]
            for hf in range(CH // 512):
                osl = slice(hf * 512, (hf + 1) * 512)
                for ct in range(4):
                    p, bp = next_ps()
                    for k in range(4):
                        kb.op("pe", lambda e, k=k: e.matmul(
                            p[:], wglu[:, k, ct * 128:(ct + 1) * 128], ygT[:, k, osl], start=(k == 0), stop=(k == 3)),
                            rd=[b_wglu, b_ygT[k]], wr=[bp])
                    kb.op("act", lambda e: e.activation(sg2[:], p[:], AF.Sigmoid, bias=bglu[:, ct:ct + 1], scale=1.0),
                          rd=[bp, b_bglu], wr=[b_sg2])
                    y2, by2 = y2s[yi_[0] % 2], b_y2s[yi_[0] % 2]
                    yi_[0] += 1
                    kb.op("dve", lambda e: e.tensor_tensor(y2[:], ygT[:, ct, osl], sg2[:], ALU.mult),
                          rd=[b_ygT[ct], b_sg2], wr=[by2])
                    kb.dma("sp", brT[ct * 128:(ct + 1) * 128, c * CH + hf * 512:c * CH + (hf + 1) * 512], y2[:], rd=[by2])

        load_u(0)
        for c in range(NCH):
            if c + 1 < NCH:
                load_u(c + 1)
            uT, b_uT = uTs[c % 2], b_uTs[c % 2]
            ygT, b_ygT = ygTs[c % 2], b_ygTs[c % 2]
            for ct in range(4):
                kb.op("pool" if ct % 2 == 0 else "dve", lambda e, ct=ct: e.tensor_copy(
                    uTd[:, ct, :, :], uT[:, ct, :].rearrange("p (j t) -> p t j", t=8)),
                    rd=[b_uT], wr=[b_uTd[ct]])
            for gb in range(8):
                p, bp = next_ps()
                for gi in range(4):
                    g = gb * 4 + gi
                    gq, g8 = g // 8, g % 8
                    r0 = 64 * (g8 // 4)
                    W_, bW = WvM[g8 % 4], b_WvM[g8 % 4]
                    for t0 in range(8):
                        kb.op("pe", lambda e, gi=gi, gq=gq, r0=r0, W_=W_, t0=t0, p=p, uT=uT: e.matmul(
                            p[:, gi * J:(gi + 1) * J], W_[r0:r0 + 64, gq, t0, :], uTd[r0:r0 + 64, gq, t0, :],
                            start=(t0 == 0), stop=(t0 == 7)), rd=[bW, b_uTd[gq]], wr=[bp])
                pv = p[:, :].rearrange("p (g j) -> p g j", j=J)
                kb.op("dve", lambda e, gb=gb, pv=pv: e.tensor_copy(Xf[:, gb * 4:(gb + 1) * 4, 1:J], pv[:, :, 0:J - 1]),
                      rd=[bp], wr=[b_Xf[gb]])
                kb.op("dve", lambda e, gb=gb, pv=pv: e.tensor_copy(Vl[:, gb * 4:(gb + 1) * 4], pv[:, :, J - 1]),
                      rd=[bp], wr=[b_Vl])
                kb.op("dve", lambda e, gb=gb: e.tensor_copy(Xf[:, gb * 4:(gb + 1) * 4, 0], Sin[:, gb * 4:(gb + 1) * 4]),
                      rd=[b_Sin], wr=[b_Xf[gb]])
                kb.op("act", lambda e, gb=gb: e.copy(Xb[:, gb * 4:(gb + 1) * 4, :], Xf[:, gb * 4:(gb + 1) * 4, :]),
                      rd=[b_Xf[gb]], wr=[b_Xb[gb]])
            while pending_glu:
                emit_glu(pending_glu.pop(0))
            for l in range(NLV):
                s_ = 1 << l
                for gb in range(8):
                    p, bp = next_ps()
                    for gi in range(4):
                        g = gb * 4 + gi
                        kb.op("pe", lambda e, gi=gi, g=g, l=l, s_=s_, p=p: e.matmul(
                            p[:, gi * J + s_:(gi + 1) * J], Rot[:, l, g, :], Xb[:, g, 0:J - s_], start=True, stop=True),
                            rd=[b_Rot, b_Xb[gb]], wr=[bp])
                    pv = p[:, :].rearrange("p (g j) -> p g j", j=J)
                    kb.op("dve", lambda e, gb=gb, pv=pv, s_=s_: e.tensor_tensor(
                        Xf[:, gb * 4:(gb + 1) * 4, s_:J], pv[:, :, s_:J], Xf[:, gb * 4:(gb + 1) * 4, s_:J], ALU.add),
                        rd=[bp, b_Xf[gb]], wr=[b_Xf[gb]])
                    kb.op("act", lambda e, gb=gb: e.copy(Xb[:, gb * 4:(gb + 1) * 4, :], Xf[:, gb * 4:(gb + 1) * 4, :]),
                          rd=[b_Xf[gb]], wr=[b_Xb[gb]])
            p, bp = next_ps()
            for g in range(32):
                kb.op("pe", lambda e, g=g, p=p: e.matmul(p[:, g:g + 1], Rot[:, 0, g, :], Xb[:, g, J - 1:J], start=True, stop=True),
                      rd=[b_Rot, b_Xb[g // 4]], wr=[bp])
            kb.op("dve", lambda e, p=p: e.tensor_tensor(Sin[:], p[:, 0:32], Vl[:], ALU.add), rd=[bp, b_Vl], wr=[b_Sin])
            for ct in range(4):
                pbk = [next_ps() for _ in range(2)]
                pv_ = [pbk[b_][0][:, :].rearrange("p (t j) -> p t j", j=J) for b_ in range(2)]
                firsts = [True, True]
                for tau in range(8):
                    for t0 in range(tau, 8):
                        bi_ = t0 // 4
                        kb.op("pe", lambda e, tau=tau, t0=t0, bi_=bi_, st=firsts[bi_]: e.matmul(
                            pv_[bi_][:, t0 % 4, :], Kc[:, ct, tau, :], uTd[:, ct, t0 - tau, :],
                            start=st, stop=False), rd=[b_Kc, b_uTd[ct]], wr=[pbk[bi_][1]])
                        firsts[bi_] = False
                for g8 in range(8):
                    g = ct * 8 + g8
                    r0 = 64 * (g8 // 4)
                    for t0 in range(8):
                        bi_ = t0 // 4
                        last = (g8 % 4 == 3 and t0 % 4 == 3)
                        kb.op("pe", lambda e, g=g, r0=r0, t0=t0, bi_=bi_, last=last: e.matmul(
                            pv_[bi_][r0:r0 + 64, t0 % 4, :], WcT[:, g, t0, :], Xb[:, g, :],
                            start=False, stop=last), rd=[b_WcT, b_Xb[g // 4]], wr=[pbk[bi_][1]])
                for bi_ in range(2):
                    p, bp = pbk[bi_]
                    kb.op("act", lambda e, p=p: e.activation(gt1[:], p[:], AF.Square), rd=[bp], wr=[b_gt1])
                    kb.op("dve", lambda e: e.tensor_scalar(gt1[:], gt1[:], 0.044715, 1.0, op0=ALU.mult, op1=ALU.add),
                          rd=[b_gt1], wr=[b_gt1])
                    kb.op("dve", lambda e, p=p: e.tensor_tensor(gt2[:], p[:], gt1[:], ALU.mult), rd=[bp, b_gt1], wr=[b_gt2])
                    kb.op("act", lambda e: e.activation(gsg[:], gt2[:], AF.Sigmoid, scale=1.5957691216057308),
                          rd=[b_gt2], wr=[b_gsg])
                    yo = ygT[:, ct, :].rearrange("p (j t) -> p t j", t=8)[:, 4 * bi_:4 * bi_ + 4, :]
                    kb.op("dve", lambda e, bi_=bi_, yo=yo: e.tensor_tensor(
                        yo, pv_[bi_], gsg[:, :].rearrange("p (t j) -> p t j", j=J), ALU.mult),
                        rd=[bp, b_gsg], wr=[b_ygT[ct]])
            pending_glu.append(c)
        while pending_glu:
            emit_glu(pending_glu.pop(0))
        phase_end()

    if "4" in phases:
        def ldw(name, src, kt_n, ncol, kp=128):
            w = sb(name, [kp, kt_n, ncol], BF16)
            bw = Buf()
            kb.dma("pool", w[:], src.rearrange("(k p) c -> p k c", p=kp), wr=[bw], max_dma_last_dim=4096)
            return w, bw
        wssm, b_wssm = ldw("wssm", w_ssm_br, 4, D)
        wattn, b_wattn = ldw("wattn", w_attn_br, 4, D, kp=64)
        wmem, b_wmem = ldw("wmem", w_mem_br, 4, D)
        wo, b_wo = ldw("wo", w_o, 8, D)
        brs = [sb("brs%d" % i, [128, 8, TT], BF16) for i in range(2)]
        b_brs = bufs(2)
        gts = [sb("gts%d" % i, [128, 24, TT], BF16) for i in range(2)]
        b_gts = bufs(2)
        xts4 = [sb("xt4%d" % i, [128, 4, D], F32) for i in range(2)]
        b_xts4 = bufs(2)
        mg = sb("mg", [128, 8, TT], BF16)
        b_mg = bufs(8)
        mm = [sb("mm%d" % i, [128, TT], F32) for i in range(6)]
        b_mm = bufs(6)

        def load4(t):
            i = t % 2
            sl = slice(t * TT, (t + 1) * TT)
            kb.dma("sp", brs[i][:, 0:4, :], brT[0:512, sl].rearrange("(k p) t -> p k t", p=128), wr=[b_brs[i]])
            kb.dma("sp", brs[i][0:64, 4:8, :], brT[512:768, sl].rearrange("(k p) t -> p k t", p=64), wr=[b_brs[i]])
            kb.dma("sp", gts[i][:], zT[3328:6400, sl].rearrange("(k p) t -> p k t", p=128), wr=[b_gts[i]])
            kb.dma("sp", xts4[i][:], x[sl, :].rearrange("(b p) c -> p b c", p=128), wr=[b_xts4[i]])
        mms = [sb("mms%d" % i, [128, 4, TT], BF16) for i in range(2)]
        b_mms = bufs(2)

        def load4b(t):
            i = t % 2
            sl = slice(t * TT, (t + 1) * TT)
            kb.dma("sp", mms[i][:], brT[768:1280, sl].rearrange("(k p) t -> p k t", p=128), wr=[b_mms[i]])

        load4(0)
        load4b(0)
        for t in range(NT):
            if t + 1 < NT:
                load4(t + 1)
                load4b(t + 1)
            i = t % 2
            br_, bbr = brs[i], b_brs[i]
            gt, bgt = gts[i], b_gts[i]
            xt, bxt = xts4[i], b_xts4[i]
            mt_, bmt = mms[i], b_mms[i]
            for ct in range(8):
                cs = slice(ct * 128, (ct + 1) * 128)
                p0, bp0 = next_ps()
                for k in range(4):
                    kb.op("pe", lambda e, k=k, cs=cs, p0=p0, br_=br_: e.matmul(
                        p0[:], wssm[:, k, cs], br_[:, k, :], start=(k == 0), stop=(k == 3)),
                        rd=[b_wssm, bbr], wr=[bp0])
                p1, bp1 = next_ps()
                for k in range(4):
                    kb.op("pe", lambda e, k=k, cs=cs, p1=p1, br_=br_: e.matmul(
                        p1[:], wattn[:, k, cs], br_[0:64, 4 + k, :], start=(k == 0), stop=(k == 3)),
                        rd=[b_wattn, bbr], wr=[bp1])
                p2, bp2 = next_ps()
                for k in range(4):
                    kb.op("pe", lambda e, k=k, cs=cs, p2=p2, mt_=mt_: e.matmul(
                        p2[:], wmem[:, k, cs], mt_[:, k, :], start=(k == 0), stop=(k == 3)),
                        rd=[b_wmem, bmt], wr=[bp2])
                a0, a1, a2 = mm[(3 * ct) % 6], mm[(3 * ct + 1) % 6], mm[(3 * ct + 2) % 6]
                ba0, ba1, ba2 = b_mm[(3 * ct) % 6], b_mm[(3 * ct + 1) % 6], b_mm[(3 * ct + 2) % 6]
                kb.op("dve", lambda e, a0=a0, p0=p0, gt=gt, ct=ct: e.tensor_tensor(a0[:], p0[:], gt[:, ct, :], ALU.mult),
                      rd=[bp0, bgt], wr=[ba0])
                kb.op("dve", lambda e, a1=a1, p1=p1, gt=gt, ct=ct: e.tensor_tensor(a1[:], p1[:], gt[:, 8 + ct, :], ALU.mult),
                      rd=[bp1, bgt], wr=[ba1])
                kb.op("dve", lambda e, a2=a2, p2=p2, gt=gt, ct=ct: e.tensor_tensor(a2[:], p2[:], gt[:, 16 + ct, :], ALU.mult),
                      rd=[bp2, bgt], wr=[ba2])
                kb.op("pool", lambda e, a0=a0, a1=a1: e.tensor_tensor(a0[:], a0[:], a1[:], ALU.add),
                      rd=[ba0, ba1], wr=[ba0])
                kb.op("pool", lambda e, a0=a0, a2=a2, ct=ct: e.tensor_tensor(mg[:, ct, :], a0[:], a2[:], ALU.add),
                      rd=[ba0, ba2], wr=[b_mg[ct]])
            for tb in range(4):
                for ch in range(2):
                    p, bp = next_ps()
                    for k in range(8):
                        kb.op("pe", lambda e, k=k, tb=tb, ch=ch, p=p: e.matmul(
                            p[:], mg[:, k, tb * 128:(tb + 1) * 128], wo[:, k, ch * 512:(ch + 1) * 512],
                            start=(k == 0), stop=(k == 7)), rd=[b_mg[k], b_wo], wr=[bp])
                    kb.op("dve", lambda e, tb=tb, ch=ch, p=p, xt=xt: e.tensor_tensor(
                        xt[:, tb, ch * 512:(ch + 1) * 512], p[:], xt[:, tb, ch * 512:(ch + 1) * 512], ALU.add),
                        rd=[bp, bxt], wr=[bxt])
            kb.dma("sp", hscr[t * TT:(t + 1) * TT, :].rearrange("(b p) c -> p b c", p=128), xt[:], rd=[bxt])
        phase_end()

    if "C" in phases:
        TC = 256
        NBC = TC // 128
        NTC = L // TC
        src = hscr if "4" in phases else x
        wup = sb("wup", [128, 8, DFF], BF16)
        b_wup = bufs(8)
        wdn = sb("wdn", [128, 32, D], BF16)
        b_wdn = bufs(32)
        b_wupc = bufs(8)
        for cc in range(8):
            kb.dma("pool", wup[:, :, cc * 512:(cc + 1) * 512],
                   w_up[:, cc * 512:(cc + 1) * 512].rearrange("(k p) c -> p k c", p=128), wr=[b_wupc[cc]],
                   max_dma_last_dim=2048)
        for f in range(32):
            kb.dma("pool", wdn[:, f, :], w_down[f * 128:(f + 1) * 128, :], wr=[b_wdn[f]],
                   max_dma_last_dim=4096)
        g2 = sb("g2", [128, D], F32)
        gf = sb("gf", [128, D], F32)
        b_g2, b_gf = Buf(), Buf()
        kb.dma("sp", g2[:], g2rep[:, :], wr=[b_g2])
        kb.dma("sp", gf[:], gfrep[:, :], wr=[b_gf])
        hts = [sb("htC%d" % i, [128, NBC, D], F32) for i in range(3)]
        b_hts = bufs(3)
        nbt2s = [sb("nbtC%d" % i, [128, NBC, D], BF16) for i in range(2)]
        b_nbt2s = bufs(2)
        n2Ts = [sb("n2T%d" % i, [128, 8, TC], BF16) for i in range(2)]
        b_n2Ts = [bufs(8) for _ in range(2)]
        hid = sb("hid", [128, 32, TC], BF16)
        b_hid = bufs(32)
        rl = [sb("rl%d" % i, [128, TC], F32) for i in range(2)]
        b_rl = bufs(2)

        def load_h(t):
            kb.dma("sp", hts[t % 3][:], src[t * TC:(t + 1) * TC, :].rearrange("(b p) c -> p b c", p=128),
                   wr=[b_hts[t % 3]])

        def normC(t):
            norm_to_T(hts[t % 3], b_hts[t % 3], g2, b_g2, nbt2s[t % 2], b_nbt2s[t % 2], n2Ts[t % 2], b_n2Ts[t % 2], nblk=NBC)

        load_h(0)
        if NTC > 1:
            load_h(1)
        normC(0)
        for t in range(NTC):
            ht, b_ht = hts[t % 3], b_hts[t % 3]
            n2T, b_n2T = n2Ts[t % 2], b_n2Ts[t % 2]
            for f in range(32):
                if f == 2 and t + 1 < NTC:
                    norm_p1(hts[(t + 1) % 3], b_hts[(t + 1) % 3], g2, b_g2, nbt2s[(t + 1) % 2], b_nbt2s[(t + 1) % 2], nblk=NBC)
                if f == 14 and t + 1 < NTC:
                    norm_p2(nbt2s[(t + 1) % 2], b_nbt2s[(t + 1) % 2], n2Ts[(t + 1) % 2], b_n2Ts[(t + 1) % 2], nblk=NBC)
                if f == 20 and t + 2 < NTC:
                    load_h(t + 2)
                p, bp = next_ps()
                for kt in range(8):
                    kb.op("pe", lambda e, kt=kt, f=f, p=p: e.matmul(
                        p[:, 0:TC], wup[:, kt, f * 128:(f + 1) * 128], n2T[:, kt, :],
                        start=(kt == 0), stop=(kt == 7)),
                        rd=[b_wupc[f // 4], b_n2T[kt]], wr=[bp])
                r, br = rl[f % 2], b_rl[f % 2]
                kb.op("act", lambda e, r=r, p=p: e.activation(r[:], p[:, 0:TC], AF.Relu), rd=[bp], wr=[br])
                kb.op("pool", lambda e, r=r, f=f: e.tensor_tensor(hid[:, f, :], r[:], r[:], ALU.mult),
                      rd=[br], wr=[b_hid[f]])
            for tb in range(NBC):
                for ch in range(2):
                    p, bp = next_ps()
                    for f in range(32):
                        kb.op("pe", lambda e, f=f, tb=tb, ch=ch, p=p: e.matmul(
                            p[:], hid[:, f, tb * 128:(tb + 1) * 128], wdn[:, f, ch * 512:(ch + 1) * 512],
                            start=(f == 0), stop=(f == 31)),
                            rd=[b_hid[f], b_wdn[f]], wr=[bp])
                    kb.op("dve", lambda e, tb=tb, ch=ch, p=p, ht=ht: e.tensor_tensor(
                        ht[:, tb, ch * 512:(ch + 1) * 512], p[:], ht[:, tb, ch * 512:(ch + 1) * 512], ALU.add),
                        rd=[bp, b_ht], wr=[b_ht])
            for tb in range(NBC):
                kb.op("act", lambda e, tb=tb, ht=ht: e.activation(junk[:], ht[:, tb, :], AF.Square,
                                                                  accum_out=ss[:, 4 + tb:5 + tb]),
                      rd=[b_ht], wr=[b_junk, b_ss])
            kb.op("act", lambda e: e.activation(rstd[:, 4:4 + NBC], ss[:, 4:4 + NBC], AF.Sqrt,
                                                bias=epsc[:, 0:1], scale=1.0 / D),
                  rd=[b_ss, b_eps], wr=[b_rstd])
            kb.op("dve", lambda e: e.reciprocal(rstd[:, 4:4 + NBC], rstd[:, 4:4 + NBC]),
                  rd=[b_rstd], wr=[b_rstd])
            for tb in range(NBC):
                kb.op("dve", lambda e, tb=tb, ht=ht: e.scalar_tensor_tensor(
                    out=ht[:, tb, :], in0=ht[:, tb, :], scalar=rstd[:, 4 + tb:5 + tb], in1=gf[:],
                    op0=ALU.mult, op1=ALU.mult), rd=[b_ht, b_rstd, b_gf], wr=[b_ht])
            kb.dma("sp", out[t * TC:(t + 1) * TC, :].rearrange("(b p) c -> p b c", p=128), ht[:],
                   rd=[b_ht])
        phase_end()

    kb.wait_all("sp", kb.all_toks())
    return nc


def _consts():
    k = np.arange(128)[:, None]
    q = np.arange(128)[None, :]
    cur = (k <= q).astype(np.float32)
    prev = (k >= q).astype(np.float32)
    m = np.concatenate([cur, prev, cur, prev], axis=1)
    sel = np.zeros((128, 64), np.float32)
    sel[64 + np.arange(64), np.arange(64)] = 1.0
    return np.ascontiguousarray(m), sel


_MASKCP, _SELF = _consts()


def _ssm_layouts(inp):
    f = np.float32
    lr = np.asarray(inp["ssm_lambda_re"][0], f); li = np.asarray(inp["ssm_lambda_im"][0], f)
    ldt = np.asarray(inp["ssm_log_dt"][0], f)
    br = np.asarray(inp["ssm_b_re"][0], f); bi = np.asarray(inp["ssm_b_im"][0], f)
    cr = np.asarray(inp["ssm_c_re"][0], f); ci = np.asarray(inp["ssm_c_im"][0], f)
    dd = np.asarray(inp["ssm_d"][0], f)
    part = np.arange(128)
    g8, h = part // 16, part % 16
    gq = np.arange(4)
    gidx = 8 * gq[None, :] + g8[:, None]
    o = {}
    o["lamC_re"] = lr[gidx]
    o["lamC_im"] = li[gidx]
    o["dtC"] = ldt[gidx]
    o["bC_re"] = br[gidx, :, h[:, None]]
    o["bC_im"] = bi[gidx, :, h[:, None]]
    o["dcolC"] = dd[gidx, h[:, None]]
    o["k7"] = np.broadcast_to((7 - np.arange(8, dtype=f))[None, :], (128, 8))
    o["k9"] = np.broadcast_to(np.arange(9, dtype=f)[None, :], (128, 9))
    o["evenodd"] = (g8[:, None] % 4 == np.arange(4)[None, :]).astype(f)
    o["bmask"] = (g8[:, None] == np.arange(8)[None, :]).astype(f)
    half, p = part // 64, part % 64
    o["lamP_re"] = lr[:, p].T
    o["lamP_im"] = li[:, p].T
    o["dtP"] = np.broadcast_to(ldt[None, :], (128, 32))
    crP = np.transpose(cr, (2, 0, 1))[p]
    ciP = np.transpose(ci, (2, 0, 1))[p]
    hm = (half == 0)[:, None, None]
    o["cU"] = np.where(hm, crP, ciP)
    o["cW"] = np.where(hm, ciP, crP)
    o["sgn"] = np.where(half == 0, 1.0, -1.0).astype(f)[:, None]
    ps_ = np.zeros((128, 128), f)
    ps_[part, (part + 64) % 128] = 1.0
    o["pswap"] = ps_
    return {k: np.ascontiguousarray(np.asarray(v, f)) for k, v in o.items()}


def host_inputs(inp, b, L=SEQ):
    f = np.float32
    rep = lambda v: np.ascontiguousarray(np.broadcast_to(np.asarray(v, f).reshape(1, -1), (128, v.size)))
    m = {
        "x": np.ascontiguousarray(np.asarray(inp["x"][b, :L], f)),
        "w_in": np.ascontiguousarray(np.asarray(inp["w_in"][0], f)),
        "g1rep": rep(np.asarray(inp["norm1_g"][0])),
        "g2rep": rep(np.asarray(inp["norm2_g"][0])),
        "gfrep": rep(np.asarray(inp["final_g"])),
        "bgT": np.ascontiguousarray(np.asarray(inp["b_gate"][0], f).reshape(24, 128).T),
        "w_up": np.ascontiguousarray(np.asarray(inp["w_up"][0], f)),
        "w_down": np.ascontiguousarray(np.asarray(inp["w_down"][0], f)),
        "ident": np.eye(128, dtype=f),
        "mem": np.ascontiguousarray(np.asarray(inp["mem"][b], f)),
        "gmrep": rep(np.asarray(inp["mem_norm_g"][0])),
        "w_mem_kv": np.ascontiguousarray(np.asarray(inp["w_mem_kv"][0], f)),
        "w_glu": np.ascontiguousarray(np.asarray(inp["w_glu"][0], f)),
        "bgluT": np.ascontiguousarray(np.asarray(inp["b_glu"][0], f).reshape(4, 128).T),
        "w_ssm_br": np.ascontiguousarray(np.asarray(inp["w_ssm_br"][0], f)),
        "w_attn_br": np.ascontiguousarray(np.asarray(inp["w_attn_br"][0], f)),
        "w_mem_br": np.ascontiguousarray(np.asarray(inp["w_mem_br"][0], f)),
        "w_o": np.ascontiguousarray(np.asarray(inp["w_o"][0], f)),
        "maskcp": _MASKCP,
        **_ssm_layouts(inp),
        "selfm": _SELF,
    }
    return m


def kernel(**inputs):
    nc = build(SEQ)
    in_maps = [host_inputs(inputs, b) for b in range(NB)]
    res = run_bass_kernel_spmd(nc, in_maps, core_ids=list(range(NB)))
    return np.stack([np.asarray(r["out"], np.float32) for r in res.results], axis=0)
```
